# Optimizing a Trainium2 kernel written in Bass

```python
import jax, jax.numpy as jnp
from jax import lax
import numpy as np

D_MODEL = 2048
BATCH = 4
SEQ = 4096
DEPTH = 2

N_MIXERS = 2
N_MLA_LAYERS = (DEPTH + 1) // 2
N_CONV_LAYERS = DEPTH // 2
MLA_HEADS = 16
QK_NOPE_DIM = 128
QK_ROPE_DIM = 64
V_HEAD_DIM = 128
Q_LORA_RANK = 512
KV_LORA_RANK = 512
MLA_LATENT_DIM = Q_LORA_RANK + KV_LORA_RANK + QK_ROPE_DIM
ROPE_THETA = 10000.0
CONV_WIDTH = 3
D_FF = 4 * D_MODEL
Q_BLOCK = 128
NORM_EPS = 1e-6
N_MOD = 6

kernel_name = "hybrid_mla_shortconv_adaln_sandwich"


def rmsnorm(x, g):
    x32 = x.astype(jnp.float32)
    y = x32 * lax.rsqrt(jnp.mean(x32 * x32, axis=-1, keepdims=True) + NORM_EPS)
    return (y * g.astype(jnp.float32)).astype(x.dtype)


def rope_cos_sin(positions):
    inv_freq = ROPE_THETA ** (-jnp.arange(0, QK_ROPE_DIM, 2, dtype=jnp.float32) / QK_ROPE_DIM)
    ang = positions.astype(jnp.float32)[..., None] * inv_freq
    return jnp.cos(ang), jnp.sin(ang)


def apply_rope(t, cos, sin):
    t32 = t.astype(jnp.float32)
    half = QK_ROPE_DIM // 2
    t1, t2 = t32[..., :half], t32[..., half:]
    out = jnp.concatenate([t1 * cos - t2 * sin, t2 * cos + t1 * sin], axis=-1)
    return out.astype(t.dtype)


def mla_mixer(h, positions, w_in, g_q, g_kv, w_uq, w_ukv, w_o):
    B, S, _ = h.shape
    lat = h @ w_in
    c_q = rmsnorm(lat[..., :Q_LORA_RANK], g_q)
    c_kv = rmsnorm(lat[..., Q_LORA_RANK:Q_LORA_RANK + KV_LORA_RANK], g_kv)
    k_rope = lat[..., Q_LORA_RANK + KV_LORA_RANK:]
    cos, sin = rope_cos_sin(positions)
    k_rope = apply_rope(k_rope, cos, sin)
    q = jnp.einsum('bsr,rhd->bshd', c_q, w_uq)
    q_nope = q[..., :QK_NOPE_DIM]
    q_rope = apply_rope(q[..., QK_NOPE_DIM:], cos[:, :, None, :], sin[:, :, None, :])
    kv = jnp.einsum('bsr,rhd->bshd', c_kv, w_ukv)
    k_nope, v = kv[..., :QK_NOPE_DIM], kv[..., QK_NOPE_DIM:]

    n_blk = S // Q_BLOCK
    scale = (QK_NOPE_DIM + QK_ROPE_DIM) ** -0.5
    qn_blocks = q_nope.reshape(B, n_blk, Q_BLOCK, MLA_HEADS, QK_NOPE_DIM).transpose(1, 0, 2, 3, 4)
    qr_blocks = q_rope.reshape(B, n_blk, Q_BLOCK, MLA_HEADS, QK_ROPE_DIM).transpose(1, 0, 2, 3, 4)
    starts = jnp.arange(n_blk, dtype=jnp.int32) * Q_BLOCK
    k_idx = jnp.arange(S, dtype=jnp.int32)

    def attend(args):
        qn, qr, start = args
        s = (jnp.einsum('bqhd,bkhd->bhqk', qn, k_nope, preferred_element_type=jnp.float32)
             + jnp.einsum('bqhd,bkd->bhqk', qr, k_rope, preferred_element_type=jnp.float32)) * scale
        q_idx = start + jnp.arange(Q_BLOCK, dtype=jnp.int32)
        causal = k_idx[None, :] <= q_idx[:, None]
        s = jnp.where(causal, s, jnp.finfo(jnp.float32).min)
        p = jax.nn.softmax(s, axis=-1)
        return jnp.einsum('bhqk,bkhd->bqhd', p.astype(v.dtype), v)

    o = lax.map(attend, (qn_blocks, qr_blocks, starts))
    o = o.transpose(1, 0, 2, 3, 4).reshape(B, S, MLA_HEADS * V_HEAD_DIM)
    return o @ w_o


def short_conv_mixer(h, w_in, conv_w, w_out):
    proj = h @ w_in
    b_gate = proj[..., :D_MODEL]
    c_gate = proj[..., D_MODEL:2 * D_MODEL]
    u = proj[..., 2 * D_MODEL:]
    z = c_gate * u
    z = lax.conv_general_dilated(z, conv_w[:, None, :].astype(z.dtype), window_strides=(1,),
                                 padding=[(CONV_WIDTH - 1, 0)],
                                 dimension_numbers=('NWC', 'WIO', 'NWC'),
                                 feature_group_count=D_MODEL)
    return (b_gate * z) @ w_out


def sq_relu_mlp(h, w_up, w_down):
    a = jax.nn.relu(h @ w_up)
    return (a * a) @ w_down


def setup_inputs(seed: int = 0) -> dict:
    key = jax.random.key(seed)
    ks = jax.random.split(key, 20)
    f32 = jnp.float32

    def nrm(k, shape, fan_in, mult=1.0):
        return jax.random.normal(k, shape, f32) * (mult * fan_in ** -0.5)

    x = jax.random.normal(ks[0], (BATCH, SEQ, D_MODEL), f32)
    c = jax.random.normal(ks[1], (BATCH, D_MODEL), f32)
    offsets = jax.random.randint(ks[2], (BATCH, 1), 0, 1024, dtype=jnp.int32)
    positions = jnp.arange(SEQ, dtype=jnp.int32)[None, :] + offsets
    w_mod = nrm(ks[3], (DEPTH, D_MODEL, N_MOD * D_MODEL), D_MODEL, 0.5)
    b_mod = 0.02 * jax.random.normal(ks[4], (DEPTH, N_MOD * D_MODEL), f32)
    norm_g = 1.0 + 0.05 * jax.random.normal(ks[5], (DEPTH, 4, D_MODEL), f32)
    mla_w_in = nrm(ks[6], (N_MLA_LAYERS, D_MODEL, MLA_LATENT_DIM), D_MODEL)
    mla_g_q = 1.0 + 0.05 * jax.random.normal(ks[7], (N_MLA_LAYERS, Q_LORA_RANK), f32)
    mla_g_kv = 1.0 + 0.05 * jax.random.normal(ks[8], (N_MLA_LAYERS, KV_LORA_RANK), f32)
    mla_w_uq = nrm(ks[9], (N_MLA_LAYERS, Q_LORA_RANK, MLA_HEADS, QK_NOPE_DIM + QK_ROPE_DIM), Q_LORA_RANK)
    mla_w_ukv = nrm(ks[10], (N_MLA_LAYERS, KV_LORA_RANK, MLA_HEADS, QK_NOPE_DIM + V_HEAD_DIM), KV_LORA_RANK)
    mla_w_o = nrm(ks[11], (N_MLA_LAYERS, MLA_HEADS * V_HEAD_DIM, D_MODEL), MLA_HEADS * V_HEAD_DIM)
    conv_w_in = nrm(ks[12], (N_CONV_LAYERS, D_MODEL, 3 * D_MODEL), D_MODEL)
    conv_w = nrm(ks[13], (N_CONV_LAYERS, CONV_WIDTH, D_MODEL), CONV_WIDTH)
    conv_w_out = nrm(ks[14], (N_CONV_LAYERS, D_MODEL, D_MODEL), D_MODEL)
    mlp_w_up = nrm(ks[15], (DEPTH, D_MODEL, D_FF), D_MODEL)
    mlp_w_down = nrm(ks[16], (DEPTH, D_FF, D_MODEL), D_FF)
    return {"x": x, "c": c, "positions": positions, "w_mod": w_mod, "b_mod": b_mod,
            "norm_g": norm_g, "mla_w_in": mla_w_in, "mla_g_q": mla_g_q, "mla_g_kv": mla_g_kv,
            "mla_w_uq": mla_w_uq, "mla_w_ukv": mla_w_ukv, "mla_w_o": mla_w_o,
            "conv_w_in": conv_w_in, "conv_w": conv_w, "conv_w_out": conv_w_out,
            "mlp_w_up": mlp_w_up, "mlp_w_down": mlp_w_down}


def reference(x, c, positions, w_mod, b_mod, norm_g, mla_w_in, mla_g_q, mla_g_kv,
              mla_w_uq, mla_w_ukv, mla_w_o, conv_w_in, conv_w, conv_w_out,
              mlp_w_up, mlp_w_down):
    cond = jax.nn.silu(c)
    for i in range(DEPTH):
        mod = (cond @ w_mod[i] + b_mod[i])[:, None, :]
        sh1, sc1, g1, sh2, sc2, g2 = jnp.split(mod, N_MOD, axis=-1)
        h = rmsnorm(x, norm_g[i, 0]) * (1.0 + sc1) + sh1
        j = i // N_MIXERS
        if i % N_MIXERS == 0:
            y = mla_mixer(h, positions, mla_w_in[j], mla_g_q[j], mla_g_kv[j],
                          mla_w_uq[j], mla_w_ukv[j], mla_w_o[j])
        else:
            y = short_conv_mixer(h, conv_w_in[j], conv_w[j], conv_w_out[j])
        x = x + g1 * rmsnorm(y, norm_g[i, 1])
        h = rmsnorm(x, norm_g[i, 2]) * (1.0 + sc2) + sh2
        y = sq_relu_mlp(h, mlp_w_up[i], mlp_w_down[i])
        x = x + g2 * rmsnorm(y, norm_g[i, 3])
    return x
```

```python
import math
from contextlib import ExitStack

import numpy as np
import ml_dtypes

import concourse.bass as bass
import concourse.mybir as mybir
from concourse.bass_utils import run_bass_kernel_spmd

F32 = mybir.dt.float32
BF16 = mybir.dt.bfloat16
I32 = mybir.dt.int32
AF = mybir.ActivationFunctionType
ALU = mybir.AluOpType

D = 2048
NKC = 16
SEQ = 4096
HALF = 2048
NH = 16
DFF = 8192
EPS = 1e-6
SCALE = (128 + 64) ** -0.5
NEG = -30000.0
NTOK = HALF + 2

ENGS = ("pe", "act", "dve", "pool", "sp")
DMA_POOL = 16
SEM_ROT = 30000
N_ROT = 3


class Buf:
    __slots__ = ("name", "w", "r")

    def __init__(self, name=""):
        self.name = name
        self.w = None
        self.r = []


class Op:
    __slots__ = ("eng", "fn", "deps", "dma", "sem", "val", "signal", "prev")

    def __init__(self, eng, fn, dma):
        self.eng = eng
        self.fn = fn
        self.deps = []
        self.dma = dma
        self.sem = None
        self.val = 0
        self.signal = dma
        self.prev = None


class _Rec:
    def __init__(self):
        self.call = None

    def __getattr__(self, name):
        def f(*a, **kw):
            self.call = (name, a, kw)
        return f


class Ctx:
    def __init__(self, nc, stack):
        self.nc = nc
        self.cnt_sems = {e: [stack.enter_context(nc.semaphore(f"c_{e}_{i}")) for i in range(N_ROT)]
                         for e in ("pe", "act", "dve", "pool")}
        self.cnt = {e: 0 for e in ENGS}
        self.dma_sems = {e: [stack.enter_context(nc.semaphore(f"d_{e}_{i}")) for i in range(DMA_POOL)]
                         for e in ("sp", "pool")}
        self.dma_i = {e: 0 for e in ENGS}
        self.dma_last = {e: {} for e in ENGS}
        self.n_ops = 0


class Prog:
    def __init__(self, ctx):
        self.ctx = ctx
        self.q = {e: [] for e in ENGS}

    def op(self, eng, fn, reads=(), writes=(), dma=False):
        rec = _Rec()
        fn(rec)
        assert rec.call is not None
        o = Op(eng, rec.call, dma)
        deps = []
        for b in reads:
            if b.w is not None:
                deps.append((b.w, True))
        for b in writes:
            if b.w is not None:
                deps.append((b.w, False))
            deps.extend((r, False) for r in b.r)
        seen = set()
        for d, raw in deps:
            if d is o:
                continue
            if d.eng == eng and not d.dma and not dma:
                if eng == "pe":
                    continue
            if id(d) in seen:
                continue
            seen.add(id(d))
            o.deps.append(d)
            d.signal = True
        for b in reads:
            if not dma:
                b.r = [x for x in b.r if x.dma or x.eng != eng]
            b.r.append(o)
        for b in writes:
            b.w = o
            b.r = []
        self.q[eng].append(o)
        return o

    def dma(self, eng, out, in_, reads=(), writes=(), **kw):
        return self.op(eng, lambda e: e.dma_start(out=out, in_=in_, **kw), reads, writes, dma=True)

    def emit(self):
        ctx = self.ctx
        nc = ctx.nc
        for e in ENGS:
            for o in self.q[e]:
                if o.dma:
                    i = ctx.dma_i[e]
                    s = i % DMA_POOL
                    o.sem = ctx.dma_sems[e][s]
                    o.val = 16 * (i // DMA_POOL + 1)
                    o.prev = ctx.dma_last[e].get(s)
                    ctx.dma_last[e][s] = (o.sem, o.val)
                    ctx.dma_i[e] = i + 1
                elif o.signal:
                    c = ctx.cnt[e]
                    assert c < SEM_ROT * N_ROT
                    o.sem = ctx.cnt_sems[e][c // SEM_ROT]
                    o.val = c % SEM_ROT + 1
                    ctx.cnt[e] = c + 1
                ctx.n_ops += 1

        def run(e, eng):
            waited = {}

            def wait(sem, val):
                k = id(sem)
                if waited.get(k, 0) >= val:
                    return
                waited[k] = val
                eng.wait_ge(sem, val)

            for o in self.q[e]:
                for d in o.deps:
                    wait(d.sem, d.val)
                if o.dma and o.prev is not None:
                    wait(*o.prev)
                name, a, kw = o.fn
                ins = getattr(eng, name)(*a, **kw)
                if o.dma:
                    ins.then_inc(o.sem, 16)
                elif o.signal:
                    ins.then_inc(o.sem, 1)
            for (sem, val) in ctx.dma_last[e].values():
                wait(sem, val)

        with nc.Block() as block:
            block.tensor(lambda eng: run("pe", eng))
            block.scalar(lambda eng: run("act", eng))
            block.vector(lambda eng: run("dve", eng))
            block.gpsimd(lambda eng: run("pool", eng))
            block.sync(lambda eng: run("sp", eng))


def build(debug=False, stop_after=99):
    nc = bass.Bass("TRN2", target_bir_lowering=False)

    uid = [0]

    def sbt(name, shape, dt):
        uid[0] += 1
        return nc.sbuf_tensor(f"{name}_{uid[0]}", shape, dt)

    def din(name, shape, dt=F32):
        return nc.dram_tensor(name, list(shape), dt, kind="ExternalInput")

    xk = din("xk", [SEQ, D])
    posT = din("posT", [128, 32], I32)
    kbias_d = din("kbias", [128, 16])
    hflag_d = din("hflag", [128, 1])
    cvec_d = din("cvec", [128, 16])
    invf_d = din("invf", [128, 32])
    cmask_d = din("cmask", [128, 4, 512])
    hmask_d = din("hmask", [128, 2])
    w_mod = din("w_mod", [2, D, 6 * D])
    b_mod = din("b_mod", [2, 6 * D])
    norm_g = din("norm_g", [2, 4 * D])
    mla_w_in = din("mla_w_in", [D, 1088])
    mla_g_q = din("mla_g_q", [512])
    mla_g_kv = din("mla_g_kv", [512])
    mla_w_uq = din("mla_w_uq", [512, NH, 192])
    mla_w_ukv = din("mla_w_ukv", [512, NH, 256])
    mla_w_o = din("mla_w_o", [D, D])
    conv_w_in = din("conv_w_in", [D, 3 * D])
    conv_wT = din("conv_wT", [128, 16, 3])
    conv_w_out = din("conv_w_out", [D, D])
    mlp_w_up = din("mlp_w_up", [2, D, DFF])
    mlp_w_down = din("mlp_w_down", [2, DFF, D])
    out = nc.dram_tensor("out", [HALF, D], F32, kind="ExternalOutput")

    okind = "ExternalOutput" if debug else "Internal"
    modv = nc.dram_tensor("modv", [12, D], F32, kind=okind)
    xr = nc.dram_tensor("xr", [NTOK, D], F32, kind=okind)
    oT_d = nc.dram_tensor("oT_d", [NH, 128, NTOK], BF16, kind=okind)
    wu_bf = nc.dram_tensor("wu_bf", [32, 128, 16 * 256], BF16)
    wd_bf = nc.dram_tensor("wd_bf", [32, 128, 8 * 512], BF16)
    wi_bf = nc.dram_tensor("wi_bf", [8, 128, 16 * 3 * 256], BF16)
    if debug:
        dbg_lat = nc.dram_tensor("dbg_lat", [128, 4, SEQ], BF16, kind="ExternalOutput")
        dbg_kr = nc.dram_tensor("dbg_kr", [64, SEQ], BF16, kind="ExternalOutput")
        dbg_cq = nc.dram_tensor("dbg_cq", [128, 4, NTOK], BF16, kind="ExternalOutput")
        dbg_x1 = nc.dram_tensor("dbg_x1", [NTOK, D], F32, kind="ExternalOutput")
        dbg_x2 = nc.dram_tensor("dbg_x2", [NTOK, D], F32, kind="ExternalOutput")
        dbg_x3 = nc.dram_tensor("dbg_x3", [HALF, D], F32, kind="ExternalOutput")

    with ExitStack() as top:
        E = top.enter_context
        ctx = Ctx(nc, top)
        ps = E(nc.psum_tensor("ps", [128, 8, 512], F32))
        PB = [Buf(f"ps{i}") for i in range(8)]
        identf = E(sbt("identf", [128, 128], F32))
        identb = E(sbt("identb", [128, 128], BF16))
        onesf = E(sbt("onesf", [128, 128], F32))
        B_const = Buf("const")

        P = Prog(ctx)
        P.op("pool", lambda e: e.memset(identf[:], 0.0), writes=[B_const])
        P.op("pool", lambda e: e.affine_select(out=identf[:], in_=identf[:], pattern=[[-1, 128]],
                                               compare_op=ALU.not_equal, fill=1.0, base=0, channel_multiplier=1),
             reads=[B_const], writes=[B_const])
        P.op("pool", lambda e: e.memset(onesf[:], 1.0), writes=[B_const])
        P.op("dve", lambda e: e.tensor_copy(out=identb[:], in_=identf[:]), reads=[B_const], writes=[B_const])
        P.emit()

        def prenorm(P, S, x_ap, nt, hT_ap, hT_buf, tb):
            prenorm_front(P, S, x_ap, nt)
            prenorm_back(P, S, nt, hT_ap, hT_buf, tb)

        def prenorm_front(P, S, x_ap, nt):
            i = S["i"]
            S["i"] += 1
            xt, xb = S["xt"][i % 2], S["xb"][i % 2]
            st, sb = S["st"][i % 2], S["sb"][i % 2]
            P.dma("sp", xt[0:nt, :], x_ap, writes=[xb])
            P.op("act", lambda e: e.activation(out=S["junk"][0:nt, :], in_=xt[0:nt, :], func=AF.Square,
                                               accum_out=st[0:nt, 0:1]),
                 reads=[xb], writes=[S["junkb"], sb])
            P.op("act", lambda e: e.activation(out=st[0:nt, 1:2], in_=st[0:nt, 0:1], func=AF.Sqrt,
                                               scale=1.0 / D, bias=S["eps"][0:nt, :]),
                 reads=[sb], writes=[sb])
            P.op("dve", lambda e: e.reciprocal(out=st[0:nt, 2:3], in_=st[0:nt, 1:2]), reads=[sb], writes=[sb])
            xs = S["xs"]
            P.op("dve", lambda e: e.tensor_scalar(out=xs[0:nt, :], in0=xt[0:nt, :], scalar1=st[0:nt, 2:3],
                                                  scalar2=None, op0=ALU.mult),
                 reads=[xb, sb], writes=[S["xsb"]])

        def prenorm_back(P, S, nt, hT_ap, hT_buf, tb):
            xs = S["xs"]
            for kc in range(NKC):
                b = tb[kc // 4]
                j = kc % 4
                P.op("pe", lambda e, b=b, j=j, kc=kc: e.transpose(
                    out=ps[:, b, j * 128:j * 128 + nt], in_=xs[0:nt, kc * 128:(kc + 1) * 128],
                    identity=identf[0:nt, 0:nt]),
                    reads=[S["xsb"], B_const], writes=[PB[b]])
            Ac, Bc = S["Acol"], S["Bcol"]
            for kc in range(NKC):
                b = tb[kc // 4]
                j = kc % 4
                if (kc // 4) % 2 == 0:
                    P.op("act", lambda e, b=b, j=j, kc=kc: e.activation(
                        out=hT_ap[:, kc, 0:nt], in_=ps[:, b, j * 128:j * 128 + nt], func=AF.Identity,
                        scale=Ac[:, kc:kc + 1], bias=Bc[:, kc:kc + 1]),
                        reads=[PB[b], S["ABb"]], writes=[hT_buf[0]])
                else:
                    P.op("dve", lambda e, b=b, j=j, kc=kc: e.tensor_scalar(
                        out=hT_ap[:, kc, 0:nt], in0=ps[:, b, j * 128:j * 128 + nt],
                        scalar1=Ac[:, kc:kc + 1], scalar2=Bc[:, kc:kc + 1], op0=ALU.mult, op1=ALU.add),
                        reads=[PB[b], S["ABb"]], writes=[hT_buf[1]])

        def norm_scratch(ph, P, sub, need_G=True):
            A = ph.enter_context
            S = {"i": 0, "j": 0}
            S["xt"] = [A(sbt(f"xt{k}", [128, D], F32)) for k in range(2)]
            S["xb"] = [Buf() for _ in range(2)]
            S["st"] = [A(sbt(f"st{k}", [128, 4], F32)) for k in range(2)]
            S["sb"] = [Buf() for _ in range(2)]
            S["junk"] = A(sbt("junk", [128, D], BF16))
            S["junkb"] = Buf()
            S["xs"] = A(sbt("xs", [128, D], F32))
            S["xsb"] = Buf()
            S["eps"] = A(sbt("epsc", [128, 1], F32))
            S["Acol"] = A(sbt("Acol", [128, NKC], F32))
            S["Bcol"] = A(sbt("Bcol", [128, NKC], F32))
            if need_G:
                S["Gbc"] = A(sbt("Gbc", [128, D], F32))
            S["ABb"] = Buf()
            S["Gb"] = Buf()
            P.op("dve", lambda e: e.memset(S["eps"][:], EPS), writes=[S["ABb"]])
            rB, rA, rG = sub * 3 + 0, sub * 3 + 1, sub * 3 + 2
            P.dma("sp", S["Acol"][:], modv.ap()[rA].rearrange("(kc p) -> p kc", p=128), writes=[S["ABb"]],
                  allow_slow_non_contiguous=True)
            P.dma("sp", S["Bcol"][:], modv.ap()[rB].rearrange("(kc p) -> p kc", p=128), writes=[S["ABb"]],
                  allow_slow_non_contiguous=True)
            if need_G:
                P.dma("sp", S["Gbc"][:], modv.ap()[rG].partition_broadcast(128), writes=[S["Gb"]])
            return S

        def postnorm(P, S, y_ap, y_bufs, nt, x_src, x_dst, dst_bufs=()):
            i = S["j"]
            S["j"] += 1
            xt, xb = S["xt"][i % 2], S["xb"][i % 2]
            st, sb = S["st"][i % 2], S["sb"][i % 2]
            P.dma("sp", xt[0:nt, :], x_src, writes=[xb])
            P.op("act", lambda e: e.activation(out=S["junk"][0:nt, :], in_=y_ap, func=AF.Square,
                                               accum_out=st[0:nt, 0:1]),
                 reads=list(y_bufs), writes=[S["junkb"], sb])
            P.op("act", lambda e: e.activation(out=st[0:nt, 1:2], in_=st[0:nt, 0:1], func=AF.Sqrt,
                                               scale=1.0 / D, bias=S["eps"][0:nt, :]),
                 reads=[sb, S["ABb"]], writes=[sb])
            P.op("dve", lambda e: e.reciprocal(out=st[0:nt, 2:3], in_=st[0:nt, 1:2]), reads=[sb], writes=[sb])
            xs = S["xs"]
            P.op("dve", lambda e: e.scalar_tensor_tensor(out=xs[0:nt, :], in0=y_ap, scalar=st[0:nt, 2:3],
                                                         in1=S["Gbc"][0:nt, :], op0=ALU.mult, op1=ALU.mult),
                 reads=list(y_bufs) + [sb, S["Gb"]], writes=[S["xsb"]])
            P.op("dve", lambda e: e.tensor_tensor(out=xt[0:nt, :], in0=xt[0:nt, :], in1=xs[0:nt, :], op=ALU.add),
                 reads=[xb, S["xsb"]], writes=[xb])
            return P.dma("pool", x_dst, xt[0:nt, :], reads=[xb], writes=list(dst_bufs))

        def load_w_resident(P, dst, src2d, ncols, buf, kstep=4):
            nk = src2d.shape[0] // 128
            v = src2d.rearrange("(kc p) n -> p kc n", p=128)
            for k0 in range(0, nk, kstep):
                P.dma("pool", dst[:, k0:k0 + kstep, :], v[:, k0:k0 + kstep, :], writes=[buf])

        XR = [Buf(f"xr{t}") for t in range(17)]

        crep = E(sbt("crep", [128, 16, 128], BF16))
        Bcrep = Buf()

        class ModCtx:
            pass

        def mod_setup(ph, nwm):
            A = ph.enter_context
            M = ModCtx()
            M.n = 0
            M.nwm = nwm
            M.wm = [A(sbt(f"wm{k}", [128, 16, 512], BF16)) for k in range(nwm)]
            M.Bwm = [Buf() for _ in range(nwm)]
            M.res = [A(sbt(f"res{k}", [128, 512], F32)) for k in range(2)]
            M.tmp = [A(sbt(f"mtmp{k}", [128, 512], F32)) for k in range(2)]
            M.bia = [A(sbt(f"mbia{k}", [128, 512], F32)) for k in range(2)]
            M.gai = [A(sbt(f"mgai{k}", [128, 512], F32)) for k in range(2)]
            M.Bres = [Buf() for _ in range(2)]
            M.Btmp = [Buf() for _ in range(2)]
            M.Bbia = [Buf() for _ in range(2)]
            M.Bgai = [Buf() for _ in range(2)]
            return M

        def mod_block(P, M, i, v, cb, b):
            n = M.n
            M.n += 1
            c0 = v * D + cb * 512
            k = n % M.nwm
            j = n % 2
            wv = w_mod.ap()[i].rearrange("(kc p) n -> p kc n", p=128)
            P.dma("pool", M.wm[k][:], wv[:, :, c0:c0 + 512], writes=[M.Bwm[k]])
            P.dma("sp", M.bia[j][:], b_mod.ap()[i, c0:c0 + 512].partition_broadcast(128), writes=[M.Bbia[j]])
            for kc in range(16):
                P.op("pe", lambda e, kc=kc: e.matmul(ps[:, b, :], lhsT=crep[:, kc, :], rhs=M.wm[k][:, kc, :],
                                                     start=(kc == 0), stop=(kc == 15)),
                     reads=[Bcrep, M.Bwm[k]], writes=[PB[b]])
            r, t = M.res[j], M.tmp[j]
            if v in (0, 3):
                P.op("dve", lambda e: e.tensor_tensor(out=r[:], in0=ps[:, b, :], in1=M.bia[j][:], op=ALU.add),
                     reads=[PB[b], M.Bbia[j]], writes=[M.Bres[j]])
            else:
                gi = {1: 0, 2: 1, 4: 2, 5: 3}[v]
                g0 = gi * D + cb * 512
                P.dma("sp", M.gai[j][:], norm_g.ap()[i, g0:g0 + 512].partition_broadcast(128), writes=[M.Bgai[j]])
                P.op("dve", lambda e: e.tensor_tensor(out=t[:], in0=ps[:, b, :], in1=M.bia[j][:], op=ALU.add),
                     reads=[PB[b], M.Bbia[j]], writes=[M.Btmp[j]])
                if v in (1, 4):
                    P.op("dve", lambda e: e.scalar_tensor_tensor(out=r[:], in0=t[:], scalar=1.0, in1=M.gai[j][:],
                                                                 op0=ALU.add, op1=ALU.mult),
                         reads=[M.Btmp[j], M.Bgai[j]], writes=[M.Bres[j]])
                else:
                    P.op("dve", lambda e: e.tensor_tensor(out=r[:], in0=t[:], in1=M.gai[j][:], op=ALU.mult),
                         reads=[M.Btmp[j], M.Bgai[j]], writes=[M.Bres[j]])
            P.dma("sp", modv.ap()[i * 6 + v:i * 6 + v + 1, cb * 512:(cb + 1) * 512], r[0:1, :], reads=[M.Bres[j]])

        mod_jobs_early = [(0, v, cb) for v in (1, 0) for cb in range(4)]
        mod_jobs_late = [(0, v, cb) for v in (2, 3, 4, 5) for cb in range(4)] + \
                        [(1, v, cb) for v in range(6) for cb in range(4)]

        with ExitStack() as ph:
            A = ph.enter_context
            P = Prog(ctx)
            cv = A(sbt("cv", [128, 16], F32))
            cs = A(sbt("cs", [128, 16], F32))
            Bcv = Buf()
            P.dma("sp", cv[:], cvec_d.ap(), writes=[Bcv])
            P.op("act", lambda e: e.activation(out=cs[:], in_=cv[:], func=AF.Silu), reads=[Bcv], writes=[Bcv])
            for kc in range(16):
                P.op("dve", lambda e, kc=kc: e.tensor_scalar(out=crep[:, kc, :], in0=onesf[:], scalar1=cs[:, kc:kc + 1],
                                                             scalar2=None, op0=ALU.mult),
                     reads=[Bcv, B_const], writes=[Bcrep])
            M = mod_setup(ph, 3)
            for n_, (i, v, cb) in enumerate(mod_jobs_early):
                mod_block(P, M, i, v, cb, n_ % 2)
            P.emit()
        if stop_after <= 1:
            return nc

        with ExitStack() as att:
            AA = att.enter_context
            c_kvT = AA(sbt("c_kvT", [128, 4, SEQ], BF16))
            k_ropeT = AA(sbt("k_ropeT", [128, SEQ], BF16))
            c_qT = AA(sbt("c_qT", [128, 4, NTOK], BF16))
            cos2T = AA(sbt("cos2T", [64, NTOK], F32))
            sin2T = AA(sbt("sin2T", [64, NTOK], F32))
            B_ckv, B_kr, B_cq, B_cs = Buf(), Buf(), Buf(), Buf()

            with ExitStack() as ph:
                A = ph.enter_context
                P = Prog(ctx)
                S = norm_scratch(ph, P, 0, need_G=False)
                w_in = A(sbt("w_in", [128, 16, 1088], BF16))
                Bw = Buf()
                load_w_resident(P, w_in, mla_w_in.ap(), 1088, Bw)
                gq = A(sbt("gq", [128, 512], F32))
                gkv = A(sbt("gkv", [128, 512], F32))
                Bg = Buf()
                P.dma("sp", gq[:], mla_g_q.ap().partition_broadcast(128), writes=[Bg])
                P.dma("sp", gkv[:], mla_g_kv.ap().partition_broadcast(128), writes=[Bg])
                pos_i = A(sbt("pos_i", [128, 32], I32))
                pos_f = A(sbt("pos_f", [128, 32], F32))
                invf = A(sbt("invf_sb", [128, 32], F32))
                ang = A(sbt("ang", [128, 32, 32], F32))
                angi = A(sbt("angi", [128, 32, 32], I32))
                angf = A(sbt("angf", [128, 32, 32], F32))
                msk = A(sbt("msk", [128, 32, 32], F32))
                COS2 = A(sbt("COS2", [128, 32, 64], F32))
                SIN2 = A(sbt("SIN2", [128, 32, 64], F32))
                SIN2S = A(sbt("SIN2S", [128, 32, 64], F32))
                Br = Buf()
                P.dma("sp", pos_i[:], posT.ap(), writes=[Br])
                P.dma("sp", invf[:], invf_d.ap(), writes=[Br])
                P.op("dve", lambda e: e.tensor_copy(out=pos_f[:], in_=pos_i[:]), reads=[Br], writes=[Br])
                for t in range(32):
                    P.op("dve", lambda e, t=t: e.tensor_scalar(out=ang[:, t, :], in0=invf[:], scalar1=pos_f[:, t:t + 1],
                                                               scalar2=None, op0=ALU.mult), reads=[Br], writes=[Br])

                def wrap(dst_trig, shift):
                    if shift != 0.0:
                        P.op("dve", lambda e: e.tensor_scalar(out=ang[:], in0=ang[:], scalar1=shift, scalar2=None, op0=ALU.add),
                             reads=[Br], writes=[Br])
                    P.op("dve", lambda e: e.tensor_copy(out=angi[:], in_=ang[:]), reads=[Br], writes=[Br])
                    P.op("dve", lambda e: e.tensor_copy(out=angf[:], in_=angi[:]), reads=[Br], writes=[Br])
                    P.op("dve", lambda e: e.tensor_tensor(out=angf[:], in0=ang[:], in1=angf[:], op=ALU.subtract), reads=[Br], writes=[Br])
                    P.op("dve", lambda e: e.tensor_single_scalar(out=msk[:], in_=angf[:], scalar=0.5, op=ALU.is_gt), reads=[Br], writes=[Br])
                    P.op("dve", lambda e: e.tensor_tensor(out=angf[:], in0=angf[:], in1=msk[:], op=ALU.subtract), reads=[Br], writes=[Br])
                    P.op("dve", lambda e: e.tensor_single_scalar(out=msk[:], in_=angf[:], scalar=-0.5, op=ALU.is_lt), reads=[Br], writes=[Br])
                    P.op("dve", lambda e: e.tensor_tensor(out=angf[:], in0=angf[:], in1=msk[:], op=ALU.add), reads=[Br], writes=[Br])
                    P.op("act", lambda e: e.activation(out=dst_trig, in_=angf[:], func=AF.Sin, scale=2.0 * math.pi * (1.0 - 2e-6)),
                         reads=[Br], writes=[Br])

                wrap(SIN2[:, :, 0:32], 0.0)
                wrap(COS2[:, :, 0:32], 0.25)
                P.op("dve", lambda e: e.tensor_copy(out=SIN2[:, :, 32:64], in_=SIN2[:, :, 0:32]), reads=[Br], writes=[Br])
                P.op("dve", lambda e: e.tensor_copy(out=COS2[:, :, 32:64], in_=COS2[:, :, 0:32]), reads=[Br], writes=[Br])
                P.op("dve", lambda e: e.tensor_copy(out=SIN2S[:, :, 32:64], in_=SIN2[:, :, 0:32]), reads=[Br], writes=[Br])
                P.op("dve", lambda e: e.tensor_scalar(out=SIN2S[:, :, 0:32], in0=SIN2[:, :, 0:32], scalar1=-1.0, scalar2=None, op0=ALU.mult),
                     reads=[Br], writes=[Br])

                P.op("dve", lambda e: e.memset(k_ropeT[64:128, :], 0.0), writes=[B_kr])
                hT = [A(sbt(f"hTa{k}", [128, 16, 128], BF16)) for k in range(2)]
                BhT = [(Buf(), Buf()) for _ in range(2)]
                st2 = [A(sbt(f"st2{k}", [128, 6], F32)) for k in range(2)]
                Bst2 = [Buf() for _ in range(2)]
                cqf = A(sbt("cqf", [128, 512], F32))
                ckvf = A(sbt("ckvf", [128, 512], F32))
                krf = A(sbt("krf", [128, 64], F32))
                krt = A(sbt("krt", [128, 64], F32))
                kro = A(sbt("kro", [128, 64], F32))
                jq = A(sbt("jq", [128, 512], BF16))
                Bcqf, Bckvf, Bkrf, Bjq = Buf(), Buf(), Buf(), Buf()
                def part_b(t):
                    doq = t >= 15
                    if True:
                        for j in range(4):
                            P.op("pe", lambda e, j=j: e.transpose(out=ps[:, 7, j * 128:(j + 1) * 128], in_=ckvf[:, j * 128:(j + 1) * 128],
                                                                   identity=identf[:]),
                                 reads=[Bckvf, B_const], writes=[PB[7]])
                        P.op("act", lambda e, t=t: e.copy(out=c_kvT[:, :, t * 128:(t + 1) * 128],
                                                           in_=ps[:, 7, :].rearrange("p (j n) -> p j n", j=4)),
                             reads=[PB[7]], writes=[B_ckv])
                        P.op("pe", lambda e: e.transpose(out=ps[0:64, 7, 0:128], in_=kro[:, :], identity=identf[:]),
                             reads=[Bkrf, B_const], writes=[PB[7]])
                        P.op("dve", lambda e, t=t: e.tensor_copy(out=k_ropeT[0:64, t * 128:(t + 1) * 128], in_=ps[0:64, 7, 0:128]),
                             reads=[PB[7]], writes=[B_kr])
                        if doq:
                            if t == 15:
                                src_sl, dst0, nn = slice(126, 128), HALF, 2
                            else:
                                src_sl, dst0, nn = slice(0, 128), (t - 16) * 128, 128
                            for j in range(4):
                                P.op("pe", lambda e, j=j: e.transpose(out=ps[:, 7, j * 128:(j + 1) * 128], in_=cqf[:, j * 128:(j + 1) * 128],
                                                                       identity=identf[:]),
                                     reads=[Bcqf, B_const], writes=[PB[7]])
                            P.op("act", lambda e, src_sl=src_sl, dst0=dst0, nn=nn: e.copy(
                                out=c_qT[:, :, dst0:dst0 + nn],
                                in_=ps[:, 7, :].rearrange("p (j n) -> p j n", j=4)[:, :, src_sl]),
                                reads=[PB[7]], writes=[B_cq])
                            P.op("pe", lambda e, t=t: e.transpose(out=ps[0:64, 7, 0:128], in_=COS2[:, t, :], identity=identf[:]),
                                 reads=[Br, B_const], writes=[PB[7]])
                            P.op("pe", lambda e, t=t: e.transpose(out=ps[0:64, 7, 128:256], in_=SIN2[:, t, :], identity=identf[:]),
                                 reads=[Br, B_const], writes=[PB[7]])
                            P.op("dve", lambda e, src_sl=src_sl, dst0=dst0, nn=nn: e.tensor_copy(
                                out=cos2T[:, dst0:dst0 + nn], in_=ps[0:64, 7, 0:128][:, src_sl]),
                                reads=[PB[7]], writes=[B_cs])
                            P.op("dve", lambda e, src_sl=src_sl, dst0=dst0, nn=nn: e.tensor_copy(
                                out=sin2T[:, dst0:dst0 + nn], in_=ps[0:64, 7, 128:256][:, src_sl]),
                                reads=[PB[7]], writes=[B_cs])
                prenorm_front(P, S, xk.ap()[0:128, :], 128)
                for t in range(32):
                    k = t % 2
                    prenorm_back(P, S, 128, hT[k], BhT[k], [0, 1, 2, 3])
                    for bi, (c0, cn) in enumerate([(0, 512), (512, 512), (1024, 64)]):
                        if bi == 0 and t < 15:
                            continue
                        b = 4 + bi
                        for kc in range(16):
                            P.op("pe", lambda e, b=b, kc=kc, k=k, c0=c0, cn=cn: e.matmul(
                                ps[:, b, 0:cn], lhsT=hT[k][:, kc, :], rhs=w_in[:, kc, c0:c0 + cn],
                                start=(kc == 0), stop=(kc == 15)),
                                reads=[BhT[k][0], BhT[k][1], Bw], writes=[PB[b]])
                    if t > 0:
                        part_b(t - 1)
                    if t + 1 < 32:
                        prenorm_front(P, S, xk.ap()[(t + 1) * 128:(t + 2) * 128, :], 128)
                    s2, bs2 = st2[k], Bst2[k]
                    doq = t >= 15
                    if doq:
                        P.op("act", lambda e, s2=s2: e.activation(out=jq[:], in_=ps[:, 4, :], func=AF.Square, accum_out=s2[:, 0:1]),
                             reads=[PB[4]], writes=[Bjq, bs2])
                    else:
                        P.op("dve", lambda e, s2=s2: e.memset(s2[:, 0:1], 1.0), writes=[bs2])
                    P.op("act", lambda e, s2=s2: e.activation(out=jq[:], in_=ps[:, 5, :], func=AF.Square, accum_out=s2[:, 1:2]),
                         reads=[PB[5]], writes=[Bjq, bs2])
                    P.op("act", lambda e, s2=s2: e.activation(out=s2[:, 2:4], in_=s2[:, 0:2], func=AF.Sqrt, scale=1.0 / 512,
                                                             bias=S["eps"][:]),
                         reads=[bs2, S["ABb"]], writes=[bs2])
                    P.op("dve", lambda e, s2=s2: e.reciprocal(out=s2[:, 4:6], in_=s2[:, 2:4]), reads=[bs2], writes=[bs2])
                    if doq:
                        P.op("dve", lambda e, s2=s2: e.scalar_tensor_tensor(out=cqf[:], in0=ps[:, 4, :], scalar=s2[:, 4:5], in1=gq[:],
                                                                           op0=ALU.mult, op1=ALU.mult),
                             reads=[PB[4], bs2, Bg], writes=[Bcqf])
                    P.op("dve", lambda e, s2=s2: e.scalar_tensor_tensor(out=ckvf[:], in0=ps[:, 5, :], scalar=s2[:, 5:6], in1=gkv[:],
                                                                       op0=ALU.mult, op1=ALU.mult),
                         reads=[PB[5], bs2, Bg], writes=[Bckvf])
                    P.op("act", lambda e: e.copy(out=krf[:], in_=ps[:, 6, 0:64]), reads=[PB[6]], writes=[Bkrf])
                    P.op("dve", lambda e, t=t: e.tensor_tensor(out=krt[:, 0:32], in0=krf[:, 32:64], in1=SIN2S[:, t, 0:32], op=ALU.mult),
                         reads=[Bkrf, Br], writes=[Bkrf])
                    P.op("dve", lambda e, t=t: e.tensor_tensor(out=krt[:, 32:64], in0=krf[:, 0:32], in1=SIN2S[:, t, 32:64], op=ALU.mult),
                         reads=[Bkrf, Br], writes=[Bkrf])
                    P.op("dve", lambda e, t=t: e.tensor_tensor(out=kro[:], in0=krf[:], in1=COS2[:, t, :], op=ALU.mult),
                         reads=[Bkrf, Br], writes=[Bkrf])
                    P.op("dve", lambda e: e.tensor_tensor(out=kro[:], in0=kro[:], in1=krt[:], op=ALU.add),
                         reads=[Bkrf], writes=[Bkrf])
                part_b(31)
                if debug:
                    P.dma("sp", dbg_lat.ap(), c_kvT[:], reads=[B_ckv])
                    P.dma("sp", dbg_kr.ap(), k_ropeT[0:64, :], reads=[B_kr])
                    P.dma("sp", dbg_cq.ap(), c_qT[:], reads=[B_cq])
                P.emit()
            if stop_after <= 2:
                return nc

            with ExitStack() as ph:
                A = ph.enter_context
                P = Prog(ctx)
                kb = A(sbt("kb", [128, 16], F32))
                cmask = A(sbt("cmask_sb", [128, 4, 512], BF16))
                hmask = A(sbt("hmask_sb", [128, 2], BF16))
                Bm = Buf()
                P.dma("sp", kb[:], kbias_d.ap(), writes=[Bm])
                P.dma("pool", cmask[:], cmask_d.ap(), writes=[Bm])
                P.dma("pool", hmask[:], hmask_d.ap(), writes=[Bm])
                wq = [A(sbt(f"wq{k}", [128, 4, 320], BF16)) for k in range(2)]
                wkv = [A(sbt(f"wkv{k}", [128, 4, 256], BF16)) for k in range(2)]
                KnT = [A(sbt(f"KnT{k}", [128, SEQ], BF16)) for k in range(2)]
                Vaug = [A(sbt(f"Vaug{k}", [128, 32, 129], BF16)) for k in range(2)]
                QnT = [A(sbt(f"QnT{k}", [128, NTOK], BF16)) for k in range(2)]
                QrT = [A(sbt(f"QrT{k}", [128, NTOK], BF16)) for k in range(2)]
                Bwq, Bwkv, BK, BV, BQn, BQr = ([Buf() for _ in range(2)] for _ in range(6))
                qt1 = A(sbt("qt1", [64, 512], F32))
                qt2 = A(sbt("qt2", [64, 512], F32))
                Bqt = Buf()
                NPT = 3
                PT = [A(sbt(f"PT{k}", [128, 512], BF16)) for k in range(NPT)]
                BPT = [Buf() for _ in range(NPT)]
                lst = [A(sbt(f"lst{k}", [128, 2], F32)) for k in range(2)]
                Blst = [Buf() for _ in range(2)]
                of = [A(sbt(f"of{k}", [128, 128], F32)) for k in range(2)]
                Bof = [Buf() for _ in range(2)]
                oTs = [A(sbt(f"oTs{k}", [128, 512], BF16)) for k in range(2)]
                BoTs = [Buf() for _ in range(2)]
                for k in range(2):
                    P.op("dve", lambda e, k=k: e.memset(Vaug[k][:, :, 128:129], 1.0), writes=[BV[k]])
                    P.op("dve", lambda e, k=k: e.memset(QrT[k][64:128, :], 0.0), writes=[BQr[k]])
                wuq = mla_w_uq.ap().rearrange("(kc p) h d -> p kc h d", p=128)
                wukv = mla_w_ukv.ap().rearrange("(kc p) h d -> p kc h d", p=128)
                pcount = [0]
                ocount = [0]
                Mm = mod_setup(ph, 2)
                late = list(mod_jobs_late)
                slot = [0]
                def load_head_weights(h_):
                    k_ = h_ % 2
                    P.dma("pool", wq[k_][:, :, 0:192], wuq[:, :, h_, :], writes=[Bwq[k_]])
                    P.dma("pool", wq[k_][:, :, 192:224], wuq[:, :, h_, 160:192], writes=[Bwq[k_]])
                    P.dma("pool", wq[k_][:, :, 224:256], wuq[:, :, h_, 128:160], writes=[Bwq[k_]])
                    P.dma("pool", wq[k_][:, :, 256:320], wuq[:, :, h_, 128:192], writes=[Bwq[k_]])
                    P.op("dve", lambda e: e.tensor_scalar(out=wq[k_][:, :, 192:224], in0=wq[k_][:, :, 192:224], scalar1=-1.0,
                                                          scalar2=None, op0=ALU.mult), reads=[Bwq[k_]], writes=[Bwq[k_]])
                    P.dma("pool", wkv[k_][:], wukv[:, :, h_, :], writes=[Bwkv[k_]])

                qblocks = [(0, 512), (512, 512), (1024, 512), (1536, 512), (HALF, 2)]

                def proj_K(h):
                    k = h % 2
                    for kbk in range(8):
                        b = 6 + (kbk % 2)
                        for kc in range(4):
                            P.op("pe", lambda e, b=b, kc=kc, kbk=kbk, k=k: e.matmul(
                                ps[:, b, :], lhsT=wkv[k][:, kc, 0:128], rhs=c_kvT[:, kc, kbk * 512:(kbk + 1) * 512],
                                start=(kc == 0), stop=(kc == 3)), reads=[Bwkv[k], B_ckv], writes=[PB[b]])
                        P.op("dve", lambda e, b=b, kbk=kbk, k=k: e.tensor_copy(out=KnT[k][:, kbk * 512:(kbk + 1) * 512], in_=ps[:, b, :]),
                             reads=[PB[b]], writes=[BK[k]])

                def proj_V(h):
                    k = h % 2
                    for g in range(8):
                        b = 6 + (g % 2)
                        for j in range(4):
                            c = 4 * g + j
                            for kc in range(4):
                                P.op("pe", lambda e, b=b, kc=kc, c=c, j=j, k=k: e.matmul(
                                    ps[:, b, j * 128:(j + 1) * 128], lhsT=c_kvT[:, kc, c * 128:(c + 1) * 128],
                                    rhs=wkv[k][:, kc, 128:256], start=(kc == 0), stop=(kc == 3)),
                                    reads=[Bwkv[k], B_ckv], writes=[PB[b]])
                        P.op("dve", lambda e, b=b, g=g, k=k: e.tensor_copy(
                            out=Vaug[k][:, 4 * g:4 * g + 4, 0:128], in_=ps[:, b, :].rearrange("p (j n) -> p j n", j=4)),
                            reads=[PB[b]], writes=[BV[k]])

                def proj_Q(h):
                    k = h % 2
                    for (q0, nq) in qblocks:
                        for kc in range(4):
                            P.op("pe", lambda e, kc=kc, q0=q0, nq=nq, k=k: e.matmul(
                                ps[:, 6, 0:nq], lhsT=wq[k][:, kc, 0:128], rhs=c_qT[:, kc, q0:q0 + nq],
                                start=(kc == 0), stop=(kc == 3)), reads=[Bwq[k], B_cq], writes=[PB[6]])
                        P.op("dve", lambda e, q0=q0, nq=nq, k=k: e.tensor_copy(out=QnT[k][:, q0:q0 + nq], in_=ps[:, 6, 0:nq]),
                             reads=[PB[6]], writes=[BQn[k]])
                        for kc in range(4):
                            P.op("pe", lambda e, kc=kc, q0=q0, nq=nq, k=k: e.matmul(
                                ps[:, 7, 0:nq], lhsT=wq[k][:, kc, 128:256], rhs=c_qT[:, kc, q0:q0 + nq],
                                start=(kc == 0), stop=(kc == 3)), reads=[Bwq[k], B_cq], writes=[PB[7]])
                        P.op("dve", lambda e, q0=q0, nq=nq: e.tensor_tensor(out=qt1[:, 0:nq], in0=ps[0:64, 7, 0:nq],
                                                                            in1=cos2T[:, q0:q0 + nq], op=ALU.mult),
                             reads=[PB[7], B_cs], writes=[Bqt])
                        for kc in range(4):
                            P.op("pe", lambda e, kc=kc, q0=q0, nq=nq, k=k: e.matmul(
                                ps[:, 7, 0:nq], lhsT=wq[k][:, kc, 192:320], rhs=c_qT[:, kc, q0:q0 + nq],
                                start=(kc == 0), stop=(kc == 3)), reads=[Bwq[k], B_cq], writes=[PB[7]])
                        P.op("dve", lambda e, q0=q0, nq=nq: e.tensor_tensor(out=qt2[:, 0:nq], in0=ps[0:64, 7, 0:nq],
                                                                            in1=sin2T[:, q0:q0 + nq], op=ALU.mult),
                             reads=[PB[7], B_cs, Bqt], writes=[Bqt])
                        P.op("dve", lambda e, q0=q0, nq=nq, k=k: e.tensor_tensor(out=QrT[k][0:64, q0:q0 + nq], in0=qt1[:, 0:nq],
                                                                                 in1=qt2[:, 0:nq], op=ALU.add),
                             reads=[Bqt], writes=[BQr[k]])

                load_head_weights(0)
                proj_K(0)
                proj_V(0)
                proj_Q(0)
                for h in range(NH):
                    k = h % 2
                    if h + 1 < NH:
                        load_head_weights(h + 1)
                    for qi, (q0, nq) in enumerate(qblocks):
                        halo = qi == 4
                        chunks = [(c, 0, ("prev", c)) for c in range(16)]
                        if halo:
                            chunks[15] = (15, 0, ("halo", 15))
                        else:
                            for c in range(4 * qi):
                                chunks.append((16 + c, 0, None))
                            for dd in range(4):
                                chunks.append((16 + 4 * qi + dd, 128 * dd, ("diag", dd)))
                        nqt = (nq + 127) // 128
                        started = [False] * 4

                        def emit_S(ci, sb):
                            c, qs, mk = chunks[ci]
                            w = nq - qs
                            diag = mk is not None and mk[0] in ("diag", "halo")
                            P.op("pe", lambda e: e.matmul(ps[:, sb, 0:w], lhsT=KnT[k][:, c * 128:(c + 1) * 128],
                                                          rhs=QnT[k][:, q0 + qs:q0 + nq], start=True, stop=False),
                                 reads=[BK[k], BQn[k]], writes=[PB[sb]])
                            P.op("pe", lambda e: e.matmul(ps[:, sb, 0:w], lhsT=k_ropeT[:, c * 128:(c + 1) * 128],
                                                          rhs=QrT[k][:, q0 + qs:q0 + nq], start=False, stop=not diag),
                                 reads=[B_kr, BQr[k]], writes=[PB[sb]])
                            if diag:
                                mrhs = hmask[:, 0:2] if mk[0] == "halo" else cmask[:, mk[1], qs:512]
                                P.op("pe", lambda e: e.matmul(ps[:, sb, 0:w], lhsT=identb[:], rhs=mrhs, start=False, stop=True),
                                     reads=[Bm, B_const], writes=[PB[sb]])

                        def emit_PV(ci, sb):
                            c, qs, mk = chunks[ci]
                            w = nq - qs
                            pi = pcount[0] % NPT
                            pcount[0] += 1
                            if mk is not None and mk[0] in ("prev", "halo"):
                                bias = kb[:, mk[1]:mk[1] + 1]
                                rd = [PB[sb], Bm]
                            else:
                                bias = 0.0
                                rd = [PB[sb]]
                            P.op("act", lambda e: e.activation(out=PT[pi][:, 0:w], in_=ps[:, sb, 0:w], func=AF.Exp,
                                                               scale=SCALE, bias=bias),
                                 reads=rd, writes=[BPT[pi]])
                            for qt in range(qs // 128, nqt):
                                m = min(128, nq - qt * 128)
                                lo = qt * 128 - qs
                                last = (ci == len(chunks) - 1) or (qt < 3 and (not halo) and ci == len(chunks) - 4 + qt)
                                P.op("pe", lambda e, qt=qt, m=m, lo=lo, last=last: e.matmul(
                                    ps[0:m, qt, 0:129], lhsT=PT[pi][:, lo:lo + m], rhs=Vaug[k][:, c, :],
                                    start=not started[qt], stop=last),
                                    reads=[BPT[pi], BV[k]], writes=[PB[qt]])
                                started[qt] = True

                        n = len(chunks)
                        emit_S(0, 4)
                        for ci in range(n):
                            if ci + 1 < n:
                                emit_S(ci + 1, 4 + ((ci + 1) % 2))
                            emit_PV(ci, 4 + (ci % 2))
                        oi = ocount[0] % 2
                        ocount[0] += 1
                        for qt in range(nqt):
                            m = min(128, nq - qt * 128)
                            li = qt % 2
                            P.op("dve", lambda e, qt=qt, m=m, li=li: e.tensor_scalar(out=lst[li][0:m, 0:1], in0=ps[0:m, qt, 128:129],
                                                                                     scalar1=1e-30, scalar2=None, op0=ALU.max),
                                 reads=[PB[qt]], writes=[Blst[li]])
                            P.op("dve", lambda e, m=m, li=li: e.reciprocal(out=lst[li][0:m, 1:2], in_=lst[li][0:m, 0:1]),
                                 reads=[Blst[li]], writes=[Blst[li]])
                            P.op("dve", lambda e, qt=qt, m=m, li=li: e.tensor_scalar(out=of[li][0:m, :], in0=ps[0:m, qt, 0:128],
                                                                                     scalar1=lst[li][0:m, 1:2], scalar2=None, op0=ALU.mult),
                                 reads=[PB[qt], Blst[li]], writes=[Bof[li]])
                            tb = 6 + (qt % 2)
                            P.op("pe", lambda e, m=m, li=li, tb=tb: e.transpose(out=ps[:, tb, 0:m], in_=of[li][0:m, :],
                                                                                identity=identf[0:m, 0:m]),
                                 reads=[Bof[li], B_const], writes=[PB[tb]])
                            P.op("act", lambda e, qt=qt, m=m, tb=tb, oi=oi: e.copy(out=oTs[oi][:, qt * 128:qt * 128 + m], in_=ps[:, tb, 0:m]),
                                 reads=[PB[tb]], writes=[BoTs[oi]])
                        P.dma("sp", oT_d.ap()[h, :, q0:q0 + nq], oTs[oi][:, 0:nq], reads=[BoTs[oi]])
                        if h + 1 < NH:
                            if qi == 0:
                                proj_K(h + 1)
                            elif qi == 1:
                                proj_V(h + 1)
                            elif qi == 2:
                                proj_Q(h + 1)
                        slot[0] += 1
                        if late and slot[0] % 2 == 0:
                            i_, v_, cb_ = late.pop(0)
                            mod_block(P, Mm, i_, v_, cb_, 6 + (Mm.n % 2))
                assert not late
                P.emit()
        if stop_after <= 3:
            return nc

        with ExitStack() as ph:
            A = ph.enter_context
            P = Prog(ctx)
            S = norm_scratch(ph, P, 0)
            w_o = A(sbt("w_o", [128, 16, D], BF16))
            Bw = Buf()
            load_w_resident(P, w_o, mla_w_o.ap(), D, Bw)
            oTt = [A(sbt(f"oTt{k}", [128, 16, 128], BF16)) for k in range(2)]
            BoTt = [Buf() for _ in range(2)]
            oview = oT_d.ap().rearrange("h p t -> p h t")
            for t in range(17):
                k = t % 2
                nt = 128 if t < 16 else 2
                t0 = t * 128
                P.dma("sp", oTt[k][:, :, 0:nt], oview[:, :, t0:t0 + nt], writes=[BoTt[k]])
                b0 = 4 * (t % 2)
                for cb in range(4):
                    for h in range(NH):
                        P.op("pe", lambda e, cb=cb, h=h, k=k, nt=nt: e.matmul(
                            ps[0:nt, b0 + cb, :], lhsT=oTt[k][:, h, 0:nt], rhs=w_o[:, h, cb * 512:(cb + 1) * 512],
                            start=(h == 0), stop=(h == NH - 1)), reads=[BoTt[k], Bw], writes=[PB[b0 + cb]])
                xsrc = xk.ap()[HALF + t0:HALF + t0 + 128, :] if t < 16 else xk.ap()[HALF - 2:HALF, :]
                postnorm(P, S, ps[0:nt, b0:b0 + 4, :].rearrange("p a b -> p (a b)"), PB[b0:b0 + 4], nt, xsrc,
                         xr.ap()[t0:t0 + nt, :], dst_bufs=[XR[t]])
            if debug:
                P.dma("sp", dbg_x1.ap(), xr.ap(), reads=XR)
            P.emit()
        if stop_after <= 4:
            return nc

        def mlp_phase(layer, groups, dst, dbg=None):
            with ExitStack() as ph:
                A = ph.enter_context
                P = Prog(ctx)
                S = norm_scratch(ph, P, 2 * layer + 1)
                hT = A(sbt("hTm", [128, 16, 512], BF16))
                aT = A(sbt("aTm", [128, 64, 512], BF16))
                ysb = A(sbt("ysb", [128, 4, D], F32))
                NWU, NWD = 3, 3
                wu = [A(sbt(f"wu{k}", [128, 16, 256], BF16)) for k in range(NWU)]
                wd = [A(sbt(f"wd{k}", [128, 8, 512], BF16)) for k in range(NWD)]
                rl = [A(sbt(f"rl{k}", [128, 512], F32)) for k in range(2)]
                BhT, BaT = (Buf(), Buf()), Buf()
                Bys = [Buf() for _ in range(4)]
                Bwu = [Buf() for _ in range(NWU)]
                Bwd = [Buf() for _ in range(NWD)]
                Brl = [Buf() for _ in range(2)]
                wup = mlp_w_up.ap()[layer].rearrange("(kc p) n -> p kc n", p=128)
                wdn = mlp_w_down.ap()[layer].rearrange("(fb p) n -> p fb n", p=128)
                nu = 0
                nd = 0
                nr = 0
                Bsu = [Buf() for _ in range(32)]
                Bsd = [Buf() for _ in range(32)]
                def do_prenorm(tiles_):
                    col_ = 0
                    for (r0_, nt_, xi_, d0_) in tiles_:
                        prenorm(P, S, xr.ap()[r0_:r0_ + nt_, :], nt_, hT[:, :, col_:col_ + nt_], BhT, [0, 1, 2, 3])
                        col_ += nt_

                do_prenorm(groups[0])
                for gi, (tiles) in enumerate(groups):
                    ntg = sum(t[1] for t in tiles)
                    for ub in range(32):
                        k = nu % NWU
                        nu += 1
                        if gi == 0:
                            P.dma("pool", wu[k][:], wup[:, :, ub * 256:(ub + 1) * 256], writes=[Bwu[k]])
                            P.dma("sp", wu_bf.ap()[ub], wu[k][:].rearrange("p a b -> p (a b)"), reads=[Bwu[k]], writes=[Bsu[ub]])
                        else:
                            P.dma("sp", wu[k][:].rearrange("p a b -> p (a b)"), wu_bf.ap()[ub], reads=[Bsu[ub]], writes=[Bwu[k]])
                        for sbk in range(2):
                            fb = ub * 2 + sbk
                            b = 4 + (fb % 4)
                            for kc in range(16):
                                P.op("pe", lambda e, b=b, kc=kc, k=k, sbk=sbk: e.matmul(
                                    ps[:, b, 0:ntg], lhsT=wu[k][:, kc, sbk * 128:(sbk + 1) * 128], rhs=hT[:, kc, 0:ntg],
                                    start=(kc == 0), stop=(kc == 15)), reads=[Bwu[k], BhT[0], BhT[1]], writes=[PB[b]])
                            ri = nr % 2
                            nr += 1
                            P.op("act", lambda e, b=b, ri=ri: e.activation(out=rl[ri][:, 0:ntg], in_=ps[:, b, 0:ntg], func=AF.Relu),
                                 reads=[PB[b]], writes=[Brl[ri]])
                            P.op("dve", lambda e, fb=fb, ri=ri: e.tensor_tensor(out=aT[:, fb, 0:ntg], in0=rl[ri][:, 0:ntg],
                                                                                in1=rl[ri][:, 0:ntg], op=ALU.mult),
                                 reads=[Brl[ri]], writes=[BaT])
                    for cb in range(4):
                        bset = 4 * (cb % 2)
                        for fg in range(8):
                            k = nd % NWD
                            nd += 1
                            si = cb * 8 + fg
                            if gi == 0:
                                P.dma("pool", wd[k][:], wdn[:, fg * 8:(fg + 1) * 8, cb * 512:(cb + 1) * 512], writes=[Bwd[k]])
                                P.dma("sp", wd_bf.ap()[si], wd[k][:].rearrange("p a b -> p (a b)"), reads=[Bwd[k]], writes=[Bsd[si]])
                            else:
                                P.dma("sp", wd[k][:].rearrange("p a b -> p (a b)"), wd_bf.ap()[si], reads=[Bsd[si]], writes=[Bwd[k]])
                            col = 0
                            for ti, (r0, nt, xi, d0) in enumerate(tiles):
                                for j in range(8):
                                    fb = fg * 8 + j
                                    P.op("pe", lambda e, ti=ti, nt=nt, col=col, fb=fb, j=j, k=k, bset=bset: e.matmul(
                                        ps[0:nt, bset + ti, :], lhsT=aT[:, fb, col:col + nt], rhs=wd[k][:, j, :],
                                        start=(fb == 0), stop=(fb == 63)), reads=[BaT, Bwd[k]], writes=[PB[bset + ti]])
                                col += nt
                        for ti, (r0, nt, xi, d0) in enumerate(tiles):
                            P.op("dve", lambda e, ti=ti, nt=nt, cb=cb, bset=bset: e.tensor_copy(
                                out=ysb[0:nt, ti, cb * 512:(cb + 1) * 512], in_=ps[0:nt, bset + ti, :]),
                                reads=[PB[bset + ti]], writes=[Bys[ti]])
                        if cb == 1 and gi + 1 < len(groups):
                            do_prenorm(groups[gi + 1])
                    for ti, (r0, nt, xi, d0) in enumerate(tiles):
                        if dst is None:
                            postnorm(P, S, ysb[0:nt, ti, :], [Bys[ti]], nt, xr.ap()[r0:r0 + nt, :], xr.ap()[r0:r0 + nt, :],
                                     dst_bufs=[XR[xi]])
                        elif d0 is not None:
                            postnorm(P, S, ysb[0:nt, ti, :], [Bys[ti]], nt, xr.ap()[r0:r0 + nt, :], dst.ap()[d0:d0 + nt, :])
                if dbg is not None:
                    P.dma("sp", dbg.ap(), xr.ap()[0:dbg.shape[0], :], reads=XR)
                P.emit()

        own_groups = [[(g * 512 + j * 128, 128, g * 4 + j, g * 512 + j * 128) for j in range(4)] for g in range(4)]
        halo_group = [[(HALF, 2, 16, None)]]
        mlp_phase(0, own_groups + halo_group, None, dbg=dbg_x2 if debug else None)
        if stop_after <= 5:
            return nc

        with ExitStack() as ph:
            A = ph.enter_context
            P = Prog(ctx)
            S = norm_scratch(ph, P, 2)
            w_out = A(sbt("w_out", [128, 16, D], BF16))
            Bw = Buf()
            load_w_resident(P, w_out, conv_w_out.ap(), D, Bw)
            cw = A(sbt("cw", [128, 16, 3], F32))
            hf = A(sbt("hf", [128, 1], F32))
            Bcw = Buf()
            P.dma("sp", cw[:], conv_wT.ap(), writes=[Bcw])
            P.dma("sp", hf[:], hflag_d.ap(), writes=[Bcw])
            hT = A(sbt("hTc", [128, 16, 514], BF16))
            gT = A(sbt("gTc", [128, 16, 512], BF16))
            BhT, BgT = (Buf(), Buf()), Buf()
            NWI = 2
            wi = [A(sbt(f"wi{k}", [128, 16, 3, 256], BF16)) for k in range(NWI)]
            Bwi = [[Buf() for _ in range(3)] for _ in range(NWI)]
            zb2 = [A(sbt(f"zb{m}", [128, 514], F32)) for m in range(2)]
            zb = [zb2[m % 2] for m in range(16)]
            Bzb2 = [Buf() for _ in range(2)]
            Bzb = [Bzb2[m % 2] for m in range(16)]
            carry = A(sbt("carry", [128, 16, 2], F32))
            Bcar = Buf()
            csb = [A(sbt(f"csb{k}", [128, 514], F32)) for k in range(2)]
            acc = [A(sbt(f"acc{k}", [128, 512], F32)) for k in range(2)]
            Bcsb = [Buf() for _ in range(2)]
            Bacc = [Buf() for _ in range(2)]
            wiv = conv_w_in.ap().rearrange("(kc p) (s n) -> p kc s n", p=128, s=3)
            nw = 0
            Bsi = [Buf() for _ in range(8)]
            def conv_prenorm(g_):
                if g_ == 0:
                    prenorm(P, S, xr.ap()[HALF:HALF + 2, :], 2, hT[:, :, 512:514], BhT, [0, 1, 2, 3])
                for j_ in range(4):
                    r0_ = g_ * 512 + j_ * 128
                    prenorm(P, S, xr.ap()[r0_:r0_ + 128, :], 128, hT[:, :, j_ * 128:(j_ + 1) * 128], BhT, [0, 1, 2, 3])

            conv_prenorm(0)
            for g in range(4):
                for m in range(16):
                    mi = m % 2
                    if mi == 0:
                        k = nw % NWI
                        nw += 1
                        if g == 0:
                            for s in range(3):
                                P.dma("pool", wi[k][:, :, s, :], wiv[:, :, s, m * 128:m * 128 + 256], writes=[Bwi[k][s]])
                            P.dma("sp", wi_bf.ap()[m // 2], wi[k][:].rearrange("p a b c -> p (a b c)"), reads=Bwi[k], writes=[Bsi[m // 2]])
                        else:
                            P.dma("sp", wi[k][:].rearrange("p a b c -> p (a b c)"), wi_bf.ap()[m // 2], reads=[Bsi[m // 2]], writes=Bwi[k])
                    pb = [4, 5, 6] if m % 2 == 0 else [7, 0, 1]
                    hb_ = 2 + (m % 2)
                    for s in range(3):
                        for kc in range(16):
                            P.op("pe", lambda e, s=s, kc=kc, k=k, b=pb[s]: e.matmul(
                                ps[:, b, 0:512], lhsT=wi[k][:, kc, s, mi * 128:(mi + 1) * 128], rhs=hT[:, kc, 0:512],
                                start=(kc == 0), stop=(kc == 15)), reads=[Bwi[k][s], BhT[0], BhT[1]], writes=[PB[pb[s]]])
                    if g == 0:
                        for s in (1, 2):
                            for kc in range(16):
                                P.op("pe", lambda e, s=s, kc=kc, k=k: e.matmul(
                                    ps[:, hb_, 2 * (s - 1):2 * s], lhsT=wi[k][:, kc, s, mi * 128:(mi + 1) * 128],
                                    rhs=hT[:, kc, 512:514],
                                    start=(kc == 0), stop=(kc == 15)), reads=[Bwi[k][s], BhT[0], BhT[1]], writes=[PB[hb_]])
                    ci = m % 2
                    if g > 0:
                        P.op("act", lambda e, m=m: e.copy(out=zb[m][:, 0:2], in_=carry[:, m, :]),
                             reads=[Bcar], writes=[Bzb[m]])
                    P.op("act", lambda e, ci=ci, b=pb[1]: e.copy(out=csb[ci][:, 2:514], in_=ps[:, b, 0:512]),
                         reads=[PB[pb[1]]], writes=[Bcsb[ci]])
                    P.op("dve", lambda e, ci=ci, b=pb[2], m=m: e.tensor_tensor(
                        out=zb[m][:, 2:514], in0=ps[:, b, 0:512], in1=csb[ci][:, 2:514], op=ALU.mult),
                        reads=[PB[pb[2]], Bcsb[ci]], writes=[Bzb[m]])
                    if g == 0:
                        P.op("act", lambda e, ci=ci: e.copy(out=csb[ci][:, 0:2], in_=ps[:, hb_, 0:2]),
                             reads=[PB[hb_]], writes=[Bcsb[ci]])
                        P.op("dve", lambda e, ci=ci, m=m: e.tensor_tensor(
                            out=zb[m][:, 0:2], in0=ps[:, hb_, 2:4], in1=csb[ci][:, 0:2], op=ALU.mult),
                            reads=[PB[hb_], Bcsb[ci]], writes=[Bzb[m]])
                        P.op("dve", lambda e, m=m: e.tensor_scalar(out=zb[m][:, 0:2], in0=zb[m][:, 0:2], scalar1=hf[:, 0:1],
                                                                   scalar2=None, op0=ALU.mult),
                             reads=[Bzb[m], Bcw], writes=[Bzb[m]])
                    ai = m % 2
                    P.op("dve", lambda e, m=m, ai=ai: e.tensor_scalar(out=acc[ai][:], in0=zb[m][:, 2:514], scalar1=cw[:, m, 2:3],
                                                                      scalar2=None, op0=ALU.mult),
                         reads=[Bzb[m], Bcw], writes=[Bacc[ai]])
                    P.op("dve", lambda e, m=m, ai=ai: e.scalar_tensor_tensor(out=acc[ai][:], in0=zb[m][:, 1:513], scalar=cw[:, m, 1:2],
                                                                             in1=acc[ai][:], op0=ALU.mult, op1=ALU.add),
                         reads=[Bzb[m], Bcw, Bacc[ai]], writes=[Bacc[ai]])
                    P.op("dve", lambda e, m=m, ai=ai: e.scalar_tensor_tensor(out=acc[ai][:], in0=zb[m][:, 0:512], scalar=cw[:, m, 0:1],
                                                                             in1=acc[ai][:], op0=ALU.mult, op1=ALU.add),
                         reads=[Bzb[m], Bcw, Bacc[ai]], writes=[Bacc[ai]])
                    P.op("dve", lambda e, m=m, ai=ai, b=pb[0]: e.tensor_tensor(out=gT[:, m, :], in0=ps[:, b, 0:512], in1=acc[ai][:], op=ALU.mult),
                         reads=[PB[pb[0]], Bacc[ai]], writes=[BgT])
                    if g < 3:
                        P.op("act", lambda e, m=m: e.copy(out=carry[:, m, :], in_=zb[m][:, 512:514]),
                             reads=[Bzb[m]], writes=[Bcar])
                for j in range(4):
                    if j == 2 and g < 3:
                        conv_prenorm(g + 1)
                    r0 = g * 512 + j * 128
                    yb0 = 4 * (j % 2)
                    for cb in range(4):
                        b = yb0 + cb
                        for m in range(16):
                            P.op("pe", lambda e, b=b, m=m, j=j, cb=cb: e.matmul(
                                ps[:, b, :], lhsT=gT[:, m, j * 128:(j + 1) * 128], rhs=w_out[:, m, cb * 512:(cb + 1) * 512],
                                start=(m == 0), stop=(m == 15)), reads=[BgT, Bw], writes=[PB[b]])
                    postnorm(P, S, ps[:, yb0:yb0 + 4, :].rearrange("p a b -> p (a b)"), PB[yb0:yb0 + 4], 128, xr.ap()[r0:r0 + 128, :],
                             xr.ap()[r0:r0 + 128, :], dst_bufs=[XR[g * 4 + j]])
            if debug:
                P.dma("sp", dbg_x3.ap(), xr.ap()[0:HALF, :], reads=XR)
            P.emit()
        if stop_after <= 6:
            return nc

        mlp_phase(1, own_groups, out)
    return nc


def make_in_maps(x, c, positions, w_mod, b_mod, norm_g, mla_w_in, mla_g_q, mla_g_kv, mla_w_uq, mla_w_ukv,
                 mla_w_o, conv_w_in, conv_w, conv_w_out, mlp_w_up, mlp_w_down):
    f32 = np.float32
    x = np.asarray(x, f32)
    c = np.asarray(c, f32)
    positions = np.asarray(positions, np.int32)
    invf = (10000.0 ** (-np.arange(0, 64, 2, dtype=np.float64) / 64) / (2 * np.pi)).astype(f32)
    invf = np.ascontiguousarray(np.broadcast_to(invf[None, :], (128, 32)))
    kk = np.arange(128)[:, None]
    qq = np.arange(512)[None, :]
    cmask = np.stack([np.where(128 * d + kk <= qq, 0.0, NEG) for d in range(4)], axis=1).astype(f32)
    hmask = np.zeros((128, 2), f32)
    hmask[127, 0] = NEG
    shared = {
        "invf": invf, "cmask": np.ascontiguousarray(cmask), "hmask": hmask,
        "w_mod": np.asarray(w_mod, f32), "b_mod": np.asarray(b_mod, f32),
        "norm_g": np.ascontiguousarray(np.asarray(norm_g, f32).reshape(2, 4 * D)),
        "mla_w_in": np.asarray(mla_w_in, f32)[0], "mla_g_q": np.asarray(mla_g_q, f32)[0],
        "mla_g_kv": np.asarray(mla_g_kv, f32)[0], "mla_w_uq": np.asarray(mla_w_uq, f32)[0],
        "mla_w_ukv": np.asarray(mla_w_ukv, f32)[0], "mla_w_o": np.asarray(mla_w_o, f32)[0],
        "conv_w_in": np.asarray(conv_w_in, f32)[0],
        "conv_wT": np.ascontiguousarray(np.asarray(conv_w, f32)[0].reshape(3, 16, 128).transpose(2, 1, 0)),
        "conv_w_out": np.asarray(conv_w_out, f32)[0],
        "mlp_w_up": np.asarray(mlp_w_up, f32), "mlp_w_down": np.asarray(mlp_w_down, f32),
    }
    maps = []
    for core in range(8):
        b, half = core // 2, core % 2
        if half == 0:
            xk = np.concatenate([np.zeros((HALF, D), f32), x[b, :HALF]], axis=0)
            pos = np.concatenate([np.zeros(HALF, np.int32), positions[b, :HALF]])
            kbias = np.full((128, 16), NEG, f32)
            hflag = np.zeros((128, 1), f32)
        else:
            xk = x[b]
            pos = positions[b]
            kbias = np.zeros((128, 16), f32)
            hflag = np.ones((128, 1), f32)
        m = dict(shared)
        m["xk"] = np.ascontiguousarray(xk)
        m["posT"] = np.ascontiguousarray(pos.reshape(32, 128).T)
        m["kbias"] = kbias
        m["hflag"] = hflag
        m["cvec"] = np.ascontiguousarray(c[b].reshape(16, 128).T)
        maps.append(m)
    return maps


def kernel(**inputs):
    maps = make_in_maps(**inputs)
    nc = build()
    res = run_bass_kernel_spmd(nc, maps, core_ids=list(range(8)))
    out = np.empty((4, SEQ, D), np.float32)
    for core in range(8):
        b, half = core // 2, core % 2
        out[b, half * HALF:(half + 1) * HALF] = np.asarray(res.results[core]["out"], np.float32)
    return out
```

```python
import math
from contextlib import ExitStack

import numpy as np
import ml_dtypes

import concourse.bass as bass
import concourse.mybir as mybir
from concourse.bass_utils import run_bass_kernel_spmd

F32 = mybir.dt.float32
BF16 = mybir.dt.bfloat16
I32 = mybir.dt.int32
AF = mybir.ActivationFunctionType
ALU = mybir.AluOpType

D = 2048
NKC = 16
SEQ = 4096
HALF = 2048
NH = 16
DFF = 8192
EPS = 1e-6
SCALE = (128 + 64) ** -0.5
NEG = -30000.0
NTOK = HALF + 2

ENGS = ("pe", "act", "dve", "pool", "sp")
DMA_POOL = 16
SEM_ROT = 30000
N_ROT = 3


class Buf:
    __slots__ = ("name", "w", "r")

    def __init__(self, name=""):
        self.name = name
        self.w = None
        self.r = []


class Op:
    __slots__ = ("eng", "fn", "deps", "dma", "sem", "val", "signal", "prev")

    def __init__(self, eng, fn, dma):
        self.eng = eng
        self.fn = fn
        self.deps = []
        self.dma = dma
        self.sem = None
        self.val = 0
        self.signal = dma
        self.prev = None


class _Rec:
    def __init__(self):
        self.call = None

    def __getattr__(self, name):
        def f(*a, **kw):
            self.call = (name, a, kw)
        return f


class Ctx:
    def __init__(self, nc, stack):
        self.nc = nc
        self.cnt_sems = {e: [stack.enter_context(nc.semaphore(f"c_{e}_{i}")) for i in range(N_ROT)]
                         for e in ("pe", "act", "dve", "pool")}
        self.cnt = {e: 0 for e in ENGS}
        self.dma_sems = {e: [stack.enter_context(nc.semaphore(f"d_{e}_{i}")) for i in range(DMA_POOL)]
                         for e in ("sp", "pool")}
        self.dma_i = {e: 0 for e in ENGS}
        self.dma_last = {e: {} for e in ENGS}
        self.n_ops = 0


class Prog:
    def __init__(self, ctx):
        self.ctx = ctx
        self.q = {e: [] for e in ENGS}

    def op(self, eng, fn, reads=(), writes=(), dma=False):
        rec = _Rec()
        fn(rec)
        assert rec.call is not None
        o = Op(eng, rec.call, dma)
        deps = []
        for b in reads:
            if b.w is not None:
                deps.append((b.w, True))
        for b in writes:
            if b.w is not None:
                deps.append((b.w, False))
            deps.extend((r, False) for r in b.r)
        seen = set()
        for d, raw in deps:
            if d is o:
                continue
            if d.eng == eng and not d.dma and not dma:
                if eng == "pe":
                    continue
            if id(d) in seen:
                continue
            seen.add(id(d))
            o.deps.append(d)
            d.signal = True
        for b in reads:
            if not dma:
                b.r = [x for x in b.r if x.dma or x.eng != eng]
            b.r.append(o)
        for b in writes:
            b.w = o
            b.r = []
        self.q[eng].append(o)
        return o

    def dma(self, eng, out, in_, reads=(), writes=(), **kw):
        return self.op(eng, lambda e: e.dma_start(out=out, in_=in_, **kw), reads, writes, dma=True)

    def emit(self):
        ctx = self.ctx
        nc = ctx.nc
        for e in ENGS:
            for o in self.q[e]:
                if o.dma:
                    i = ctx.dma_i[e]
                    s = i % DMA_POOL
                    o.sem = ctx.dma_sems[e][s]
                    o.val = 16 * (i // DMA_POOL + 1)
                    o.prev = ctx.dma_last[e].get(s)
                    ctx.dma_last[e][s] = (o.sem, o.val)
                    ctx.dma_i[e] = i + 1
                elif o.signal:
                    c = ctx.cnt[e]
                    assert c < SEM_ROT * N_ROT
                    o.sem = ctx.cnt_sems[e][c // SEM_ROT]
                    o.val = c % SEM_ROT + 1
                    ctx.cnt[e] = c + 1
                ctx.n_ops += 1

        def run(e, eng):
            waited = {}

            def wait(sem, val):
                k = id(sem)
                if waited.get(k, 0) >= val:
                    return
                waited[k] = val
                eng.wait_ge(sem, val)

            for o in self.q[e]:
                for d in o.deps:
                    wait(d.sem, d.val)
                if o.dma and o.prev is not None:
                    wait(*o.prev)
                name, a, kw = o.fn
                ins = getattr(eng, name)(*a, **kw)
                if o.dma:
                    ins.then_inc(o.sem, 16)
                elif o.signal:
                    ins.then_inc(o.sem, 1)
            for (sem, val) in ctx.dma_last[e].values():
                wait(sem, val)

        with nc.Block() as block:
            block.tensor(lambda eng: run("pe", eng))
            block.scalar(lambda eng: run("act", eng))
            block.vector(lambda eng: run("dve", eng))
            block.gpsimd(lambda eng: run("pool", eng))
            block.sync(lambda eng: run("sp", eng))


def build(debug=False, stop_after=99):
    nc = bass.Bass("TRN2", target_bir_lowering=False)

    uid = [0]

    def sbt(name, shape, dt):
        uid[0] += 1
        return nc.sbuf_tensor(f"{name}_{uid[0]}", shape, dt)

    def din(name, shape, dt=F32):
        return nc.dram_tensor(name, list(shape), dt, kind="ExternalInput")

    xk = din("xk", [SEQ, D])
    posT = din("posT", [128, 32], I32)
    kbias_d = din("kbias", [128, 16])
    hflag_d = din("hflag", [128, 1])
    cvec_d = din("cvec", [128, 16])
    invf_d = din("invf", [128, 32])
    cmask_d = din("cmask", [128, 4, 512])
    hmask_d = din("hmask", [128, 2])
    w_mod = din("w_mod", [2, D, 6 * D])
    b_mod = din("b_mod", [2, 6 * D])
    norm_g = din("norm_g", [2, 4 * D])
    mla_w_in = din("mla_w_in", [D, 1088])
    mla_g_q = din("mla_g_q", [512])
    mla_g_kv = din("mla_g_kv", [512])
    mla_w_uq = din("mla_w_uq", [512, NH, 192])
    mla_w_ukv = din("mla_w_ukv", [512, NH, 256])
    mla_w_o = din("mla_w_o", [D, D])
    conv_w_in = din("conv_w_in", [D, 3 * D])
    conv_wT = din("conv_wT", [128, 16, 3])
    conv_w_out = din("conv_w_out", [D, D])
    mlp_w_up = din("mlp_w_up", [2, D, DFF])
    mlp_w_down = din("mlp_w_down", [2, DFF, D])
    out = nc.dram_tensor("out", [HALF, D], F32, kind="ExternalOutput")

    okind = "ExternalOutput" if debug else "Internal"
    modv = nc.dram_tensor("modv", [12, D], F32, kind=okind)
    xr = nc.dram_tensor("xr", [NTOK, D], F32, kind=okind)
    oT_d = nc.dram_tensor("oT_d", [NH, 128, NTOK], BF16, kind=okind)
    wu_bf = nc.dram_tensor("wu_bf", [32, 128, 16 * 256], BF16)
    wd_bf = nc.dram_tensor("wd_bf", [32, 128, 8 * 512], BF16)
    wi_bf = nc.dram_tensor("wi_bf", [8, 128, 16 * 3 * 256], BF16)
    if debug:
        dbg_lat = nc.dram_tensor("dbg_lat", [128, 4, SEQ], BF16, kind="ExternalOutput")
        dbg_kr = nc.dram_tensor("dbg_kr", [64, SEQ], BF16, kind="ExternalOutput")
        dbg_cq = nc.dram_tensor("dbg_cq", [128, 4, NTOK], BF16, kind="ExternalOutput")
        dbg_x1 = nc.dram_tensor("dbg_x1", [NTOK, D], F32, kind="ExternalOutput")
        dbg_x2 = nc.dram_tensor("dbg_x2", [NTOK, D], F32, kind="ExternalOutput")
        dbg_x3 = nc.dram_tensor("dbg_x3", [HALF, D], F32, kind="ExternalOutput")

    with ExitStack() as top:
        E = top.enter_context
        ctx = Ctx(nc, top)
        ps = E(nc.psum_tensor("ps", [128, 8, 512], F32))
        PB = [Buf(f"ps{i}") for i in range(8)]
        identf = E(sbt("identf", [128, 128], F32))
        identb = E(sbt("identb", [128, 128], BF16))
        onesf = E(sbt("onesf", [128, 128], F32))
        B_const = Buf("const")

        P = Prog(ctx)
        P.op("pool", lambda e: e.memset(identf[:], 0.0), writes=[B_const])
        P.op("pool", lambda e: e.affine_select(out=identf[:], in_=identf[:], pattern=[[-1, 128]],
                                               compare_op=ALU.not_equal, fill=1.0, base=0, channel_multiplier=1),
             reads=[B_const], writes=[B_const])
        P.op("pool", lambda e: e.memset(onesf[:], 1.0), writes=[B_const])
        P.op("dve", lambda e: e.tensor_copy(out=identb[:], in_=identf[:]), reads=[B_const], writes=[B_const])
        P.emit()

        def prenorm(P, S, x_ap, nt, hT_ap, hT_buf, tb):
            prenorm_front(P, S, x_ap, nt)
            prenorm_back(P, S, nt, hT_ap, hT_buf, tb)

        def prenorm_front(P, S, x_ap, nt):
            i = S["i"]
            S["i"] += 1
            xt, xb = S["xt"][i % 2], S["xb"][i % 2]
            st, sb = S["st"][i % 2], S["sb"][i % 2]
            P.dma("sp", xt[0:nt, :], x_ap, writes=[xb])
            P.op("act", lambda e: e.activation(out=S["junk"][0:nt, :], in_=xt[0:nt, :], func=AF.Square,
                                               accum_out=st[0:nt, 0:1]),
                 reads=[xb], writes=[S["junkb"], sb])
            P.op("act", lambda e: e.activation(out=st[0:nt, 1:2], in_=st[0:nt, 0:1], func=AF.Sqrt,
                                               scale=1.0 / D, bias=S["eps"][0:nt, :]),
                 reads=[sb], writes=[sb])
            P.op("dve", lambda e: e.reciprocal(out=st[0:nt, 2:3], in_=st[0:nt, 1:2]), reads=[sb], writes=[sb])
            xs = S["xs"]
            P.op("dve", lambda e: e.tensor_scalar(out=xs[0:nt, :], in0=xt[0:nt, :], scalar1=st[0:nt, 2:3],
                                                  scalar2=None, op0=ALU.mult),
                 reads=[xb, sb], writes=[S["xsb"]])

        def prenorm_back(P, S, nt, hT_ap, hT_buf, tb):
            xs = S["xs"]
            for kc in range(NKC):
                b = tb[kc // 4]
                j = kc % 4
                P.op("pe", lambda e, b=b, j=j, kc=kc: e.transpose(
                    out=ps[:, b, j * 128:j * 128 + nt], in_=xs[0:nt, kc * 128:(kc + 1) * 128],
                    identity=identf[0:nt, 0:nt]),
                    reads=[S["xsb"], B_const], writes=[PB[b]])
            Ac, Bc = S["Acol"], S["Bcol"]
            for kc in range(NKC):
                b = tb[kc // 4]
                j = kc % 4
                if (kc // 4) % 2 == 0:
                    P.op("act", lambda e, b=b, j=j, kc=kc: e.activation(
                        out=hT_ap[:, kc, 0:nt], in_=ps[:, b, j * 128:j * 128 + nt], func=AF.Identity,
                        scale=Ac[:, kc:kc + 1], bias=Bc[:, kc:kc + 1]),
                        reads=[PB[b], S["ABb"]], writes=[hT_buf[0]])
                else:
                    P.op("dve", lambda e, b=b, j=j, kc=kc: e.tensor_scalar(
                        out=hT_ap[:, kc, 0:nt], in0=ps[:, b, j * 128:j * 128 + nt],
                        scalar1=Ac[:, kc:kc + 1], scalar2=Bc[:, kc:kc + 1], op0=ALU.mult, op1=ALU.add),
                        reads=[PB[b], S["ABb"]], writes=[hT_buf[1]])

        def norm_scratch(ph, P, sub, need_G=True):
            A = ph.enter_context
            S = {"i": 0, "j": 0}
            S["xt"] = [A(sbt(f"xt{k}", [128, D], F32)) for k in range(2)]
            S["xb"] = [Buf() for _ in range(2)]
            S["st"] = [A(sbt(f"st{k}", [128, 4], F32)) for k in range(2)]
            S["sb"] = [Buf() for _ in range(2)]
            S["junk"] = A(sbt("junk", [128, D], BF16))
            S["junkb"] = Buf()
            S["xs"] = A(sbt("xs", [128, D], F32))
            S["xsb"] = Buf()
            S["eps"] = A(sbt("epsc", [128, 1], F32))
            S["Acol"] = A(sbt("Acol", [128, NKC], F32))
            S["Bcol"] = A(sbt("Bcol", [128, NKC], F32))
            if need_G:
                S["Gbc"] = A(sbt("Gbc", [128, D], F32))
            S["ABb"] = Buf()
            S["Gb"] = Buf()
            P.op("dve", lambda e: e.memset(S["eps"][:], EPS), writes=[S["ABb"]])
            rB, rA, rG = sub * 3 + 0, sub * 3 + 1, sub * 3 + 2
            P.dma("sp", S["Acol"][:], modv.ap()[rA].rearrange("(kc p) -> p kc", p=128), writes=[S["ABb"]],
                  allow_slow_non_contiguous=True)
            P.dma("sp", S["Bcol"][:], modv.ap()[rB].rearrange("(kc p) -> p kc", p=128), writes=[S["ABb"]],
                  allow_slow_non_contiguous=True)
            if need_G:
                P.dma("sp", S["Gbc"][:], modv.ap()[rG].partition_broadcast(128), writes=[S["Gb"]])
            return S

        def postnorm(P, S, y_ap, y_bufs, nt, x_src, x_dst, dst_bufs=()):
            i = S["j"]
            S["j"] += 1
            xt, xb = S["xt"][i % 2], S["xb"][i % 2]
            st, sb = S["st"][i % 2], S["sb"][i % 2]
            P.dma("sp", xt[0:nt, :], x_src, writes=[xb])
            P.op("act", lambda e: e.activation(out=S["junk"][0:nt, :], in_=y_ap, func=AF.Square,
                                               accum_out=st[0:nt, 0:1]),
                 reads=list(y_bufs), writes=[S["junkb"], sb])
            P.op("act", lambda e: e.activation(out=st[0:nt, 1:2], in_=st[0:nt, 0:1], func=AF.Sqrt,
                                               scale=1.0 / D, bias=S["eps"][0:nt, :]),
                 reads=[sb, S["ABb"]], writes=[sb])
            P.op("dve", lambda e: e.reciprocal(out=st[0:nt, 2:3], in_=st[0:nt, 1:2]), reads=[sb], writes=[sb])
            xs = S["xs"]
            P.op("dve", lambda e: e.scalar_tensor_tensor(out=xs[0:nt, :], in0=y_ap, scalar=st[0:nt, 2:3],
                                                         in1=S["Gbc"][0:nt, :], op0=ALU.mult, op1=ALU.mult),
                 reads=list(y_bufs) + [sb, S["Gb"]], writes=[S["xsb"]])
            P.op("dve", lambda e: e.tensor_tensor(out=xt[0:nt, :], in0=xt[0:nt, :], in1=xs[0:nt, :], op=ALU.add),
                 reads=[xb, S["xsb"]], writes=[xb])
            return P.dma("pool", x_dst, xt[0:nt, :], reads=[xb], writes=list(dst_bufs))

        def load_w_resident(P, dst, src2d, ncols, buf, kstep=4):
            nk = src2d.shape[0] // 128
            v = src2d.rearrange("(kc p) n -> p kc n", p=128)
            for k0 in range(0, nk, kstep):
                P.dma("pool", dst[:, k0:k0 + kstep, :], v[:, k0:k0 + kstep, :], writes=[buf])

        XR = [Buf(f"xr{t}") for t in range(17)]

        crep = E(sbt("crep", [128, 16, 128], BF16))
        Bcrep = Buf()

        class ModCtx:
            pass

        def mod_setup(ph, nwm):
            A = ph.enter_context
            M = ModCtx()
            M.n = 0
            M.nwm = nwm
            M.wm = [A(sbt(f"wm{k}", [128, 16, 512], BF16)) for k in range(nwm)]
            M.Bwm = [Buf() for _ in range(nwm)]
            M.res = [A(sbt(f"res{k}", [128, 512], F32)) for k in range(2)]
            M.tmp = [A(sbt(f"mtmp{k}", [128, 512], F32)) for k in range(2)]
            M.bia = [A(sbt(f"mbia{k}", [128, 512], F32)) for k in range(2)]
            M.gai = [A(sbt(f"mgai{k}", [128, 512], F32)) for k in range(2)]
            M.Bres = [Buf() for _ in range(2)]
            M.Btmp = [Buf() for _ in range(2)]
            M.Bbia = [Buf() for _ in range(2)]
            M.Bgai = [Buf() for _ in range(2)]
            return M

        def mod_block(P, M, i, v, cb, b):
            n = M.n
            M.n += 1
            c0 = v * D + cb * 512
            k = n % M.nwm
            j = n % 2
            wv = w_mod.ap()[i].rearrange("(kc p) n -> p kc n", p=128)
            P.dma("pool", M.wm[k][:], wv[:, :, c0:c0 + 512], writes=[M.Bwm[k]])
            P.dma("sp", M.bia[j][:], b_mod.ap()[i, c0:c0 + 512].partition_broadcast(128), writes=[M.Bbia[j]])
            for kc in range(16):
                P.op("pe", lambda e, kc=kc: e.matmul(ps[:, b, :], lhsT=crep[:, kc, :], rhs=M.wm[k][:, kc, :],
                                                     start=(kc == 0), stop=(kc == 15)),
                     reads=[Bcrep, M.Bwm[k]], writes=[PB[b]])
            r, t = M.res[j], M.tmp[j]
            if v in (0, 3):
                P.op("dve", lambda e: e.tensor_tensor(out=r[:], in0=ps[:, b, :], in1=M.bia[j][:], op=ALU.add),
                     reads=[PB[b], M.Bbia[j]], writes=[M.Bres[j]])
            else:
                gi = {1: 0, 2: 1, 4: 2, 5: 3}[v]
                g0 = gi * D + cb * 512
                P.dma("sp", M.gai[j][:], norm_g.ap()[i, g0:g0 + 512].partition_broadcast(128), writes=[M.Bgai[j]])
                P.op("dve", lambda e: e.tensor_tensor(out=t[:], in0=ps[:, b, :], in1=M.bia[j][:], op=ALU.add),
                     reads=[PB[b], M.Bbia[j]], writes=[M.Btmp[j]])
                if v in (1, 4):
                    P.op("dve", lambda e: e.scalar_tensor_tensor(out=r[:], in0=t[:], scalar=1.0, in1=M.gai[j][:],
                                                                 op0=ALU.add, op1=ALU.mult),
                         reads=[M.Btmp[j], M.Bgai[j]], writes=[M.Bres[j]])
                else:
                    P.op("dve", lambda e: e.tensor_tensor(out=r[:], in0=t[:], in1=M.gai[j][:], op=ALU.mult),
                         reads=[M.Btmp[j], M.Bgai[j]], writes=[M.Bres[j]])
            P.dma("sp", modv.ap()[i * 6 + v:i * 6 + v + 1, cb * 512:(cb + 1) * 512], r[0:1, :], reads=[M.Bres[j]])

        mod_jobs_early = [(0, v, cb) for v in (1, 0) for cb in range(4)]
        mod_jobs_late = [(0, v, cb) for v in (2, 3, 4, 5) for cb in range(4)] + \
                        [(1, v, cb) for v in range(6) for cb in range(4)]

        with ExitStack() as ph:
            A = ph.enter_context
            P = Prog(ctx)
            cv = A(sbt("cv", [128, 16], F32))
            cs = A(sbt("cs", [128, 16], F32))
            Bcv = Buf()
            P.dma("sp", cv[:], cvec_d.ap(), writes=[Bcv])
            P.op("act", lambda e: e.activation(out=cs[:], in_=cv[:], func=AF.Silu), reads=[Bcv], writes=[Bcv])
            for kc in range(16):
                P.op("dve", lambda e, kc=kc: e.tensor_scalar(out=crep[:, kc, :], in0=onesf[:], scalar1=cs[:, kc:kc + 1],
                                                             scalar2=None, op0=ALU.mult),
                     reads=[Bcv, B_const], writes=[Bcrep])
            M = mod_setup(ph, 3)
            for n_, (i, v, cb) in enumerate(mod_jobs_early):
                mod_block(P, M, i, v, cb, n_ % 2)
            P.emit()
        if stop_after <= 1:
            return nc

        with ExitStack() as att:
            AA = att.enter_context
            c_kvT = AA(sbt("c_kvT", [128, 4, SEQ], BF16))
            k_ropeT = AA(sbt("k_ropeT", [128, SEQ], BF16))
            c_qT = AA(sbt("c_qT", [128, 4, NTOK], BF16))
            cos2T = AA(sbt("cos2T", [64, NTOK], F32))
            sin2T = AA(sbt("sin2T", [64, NTOK], F32))
            B_ckv, B_kr, B_cq, B_cs = Buf(), Buf(), Buf(), Buf()

            with ExitStack() as ph:
                A = ph.enter_context
                P = Prog(ctx)
                S = norm_scratch(ph, P, 0, need_G=False)
                w_in = A(sbt("w_in", [128, 16, 1088], BF16))
                Bw = Buf()
                load_w_resident(P, w_in, mla_w_in.ap(), 1088, Bw)
                gq = A(sbt("gq", [128, 512], F32))
                gkv = A(sbt("gkv", [128, 512], F32))
                Bg = Buf()
                P.dma("sp", gq[:], mla_g_q.ap().partition_broadcast(128), writes=[Bg])
                P.dma("sp", gkv[:], mla_g_kv.ap().partition_broadcast(128), writes=[Bg])
                pos_i = A(sbt("pos_i", [128, 32], I32))
                pos_f = A(sbt("pos_f", [128, 32], F32))
                invf = A(sbt("invf_sb", [128, 32], F32))
                ang = A(sbt("ang", [128, 32, 32], F32))
                angi = A(sbt("angi", [128, 32, 32], I32))
                angf = A(sbt("angf", [128, 32, 32], F32))
                msk = A(sbt("msk", [128, 32, 32], F32))
                COS2 = A(sbt("COS2", [128, 32, 64], F32))
                SIN2 = A(sbt("SIN2", [128, 32, 64], F32))
                SIN2S = A(sbt("SIN2S", [128, 32, 64], F32))
                Br = Buf()
                P.dma("sp", pos_i[:], posT.ap(), writes=[Br])
                P.dma("sp", invf[:], invf_d.ap(), writes=[Br])
                P.op("dve", lambda e: e.tensor_copy(out=pos_f[:], in_=pos_i[:]), reads=[Br], writes=[Br])
                for t in range(32):
                    P.op("dve", lambda e, t=t: e.tensor_scalar(out=ang[:, t, :], in0=invf[:], scalar1=pos_f[:, t:t + 1],
                                                               scalar2=None, op0=ALU.mult), reads=[Br], writes=[Br])

                def wrap(dst_trig, shift):
                    if shift != 0.0:
                        P.op("dve", lambda e: e.tensor_scalar(out=ang[:], in0=ang[:], scalar1=shift, scalar2=None, op0=ALU.add),
                             reads=[Br], writes=[Br])
                    P.op("dve", lambda e: e.tensor_copy(out=angi[:], in_=ang[:]), reads=[Br], writes=[Br])
                    P.op("dve", lambda e: e.tensor_copy(out=angf[:], in_=angi[:]), reads=[Br], writes=[Br])
                    P.op("dve", lambda e: e.tensor_tensor(out=angf[:], in0=ang[:], in1=angf[:], op=ALU.subtract), reads=[Br], writes=[Br])
                    P.op("dve", lambda e: e.tensor_single_scalar(out=msk[:], in_=angf[:], scalar=0.5, op=ALU.is_gt), reads=[Br], writes=[Br])
                    P.op("dve", lambda e: e.tensor_tensor(out=angf[:], in0=angf[:], in1=msk[:], op=ALU.subtract), reads=[Br], writes=[Br])
                    P.op("dve", lambda e: e.tensor_single_scalar(out=msk[:], in_=angf[:], scalar=-0.5, op=ALU.is_lt), reads=[Br], writes=[Br])
                    P.op("dve", lambda e: e.tensor_tensor(out=angf[:], in0=angf[:], in1=msk[:], op=ALU.add), reads=[Br], writes=[Br])
                    P.op("act", lambda e: e.activation(out=dst_trig, in_=angf[:], func=AF.Sin, scale=2.0 * math.pi * (1.0 - 2e-6)),
                         reads=[Br], writes=[Br])

                wrap(SIN2[:, :, 0:32], 0.0)
                wrap(COS2[:, :, 0:32], 0.25)
                P.op("dve", lambda e: e.tensor_copy(out=SIN2[:, :, 32:64], in_=SIN2[:, :, 0:32]), reads=[Br], writes=[Br])
                P.op("dve", lambda e: e.tensor_copy(out=COS2[:, :, 32:64], in_=COS2[:, :, 0:32]), reads=[Br], writes=[Br])
                P.op("dve", lambda e: e.tensor_copy(out=SIN2S[:, :, 32:64], in_=SIN2[:, :, 0:32]), reads=[Br], writes=[Br])
                P.op("dve", lambda e: e.tensor_scalar(out=SIN2S[:, :, 0:32], in0=SIN2[:, :, 0:32], scalar1=-1.0, scalar2=None, op0=ALU.mult),
                     reads=[Br], writes=[Br])

                P.op("dve", lambda e: e.memset(k_ropeT[64:128, :], 0.0), writes=[B_kr])
                hT = [A(sbt(f"hTa{k}", [128, 16, 128], BF16)) for k in range(2)]
                BhT = [(Buf(), Buf()) for _ in range(2)]
                st2 = [A(sbt(f"st2{k}", [128, 6], F32)) for k in range(2)]
                Bst2 = [Buf() for _ in range(2)]
                cqf = A(sbt("cqf", [128, 512], F32))
                ckvf = A(sbt("ckvf", [128, 512], F32))
                krf = A(sbt("krf", [128, 64], F32))
                krt = A(sbt("krt", [128, 64], F32))
                kro = A(sbt("kro", [128, 64], F32))
                jq = A(sbt("jq", [128, 512], BF16))
                Bcqf, Bckvf, Bkrf, Bjq = Buf(), Buf(), Buf(), Buf()
                def part_b(t):
                    doq = t >= 15
                    if True:
                        for j in range(4):
                            P.op("pe", lambda e, j=j: e.transpose(out=ps[:, 7, j * 128:(j + 1) * 128], in_=ckvf[:, j * 128:(j + 1) * 128],
                                                                   identity=identf[:]),
                                 reads=[Bckvf, B_const], writes=[PB[7]])
                        P.op("act", lambda e, t=t: e.copy(out=c_kvT[:, :, t * 128:(t + 1) * 128],
                                                           in_=ps[:, 7, :].rearrange("p (j n) -> p j n", j=4)),
                             reads=[PB[7]], writes=[B_ckv])
                        P.op("pe", lambda e: e.transpose(out=ps[0:64, 7, 0:128], in_=kro[:, :], identity=identf[:]),
                             reads=[Bkrf, B_const], writes=[PB[7]])
                        P.op("dve", lambda e, t=t: e.tensor_copy(out=k_ropeT[0:64, t * 128:(t + 1) * 128], in_=ps[0:64, 7, 0:128]),
                             reads=[PB[7]], writes=[B_kr])
                        if doq:
                            if t == 15:
                                src_sl, dst0, nn = slice(126, 128), HALF, 2
                            else:
                                src_sl, dst0, nn = slice(0, 128), (t - 16) * 128, 128
                            for j in range(4):
                                P.op("pe", lambda e, j=j: e.transpose(out=ps[:, 7, j * 128:(j + 1) * 128], in_=cqf[:, j * 128:(j + 1) * 128],
                                                                       identity=identf[:]),
                                     reads=[Bcqf, B_const], writes=[PB[7]])
                            P.op("act", lambda e, src_sl=src_sl, dst0=dst0, nn=nn: e.copy(
                                out=c_qT[:, :, dst0:dst0 + nn],
                                in_=ps[:, 7, :].rearrange("p (j n) -> p j n", j=4)[:, :, src_sl]),
                                reads=[PB[7]], writes=[B_cq])
                            P.op("pe", lambda e, t=t: e.transpose(out=ps[0:64, 7, 0:128], in_=COS2[:, t, :], identity=identf[:]),
                                 reads=[Br, B_const], writes=[PB[7]])
                            P.op("pe", lambda e, t=t: e.transpose(out=ps[0:64, 7, 128:256], in_=SIN2[:, t, :], identity=identf[:]),
                                 reads=[Br, B_const], writes=[PB[7]])
                            P.op("dve", lambda e, src_sl=src_sl, dst0=dst0, nn=nn: e.tensor_copy(
                                out=cos2T[:, dst0:dst0 + nn], in_=ps[0:64, 7, 0:128][:, src_sl]),
                                reads=[PB[7]], writes=[B_cs])
                            P.op("dve", lambda e, src_sl=src_sl, dst0=dst0, nn=nn: e.tensor_copy(
                                out=sin2T[:, dst0:dst0 + nn], in_=ps[0:64, 7, 128:256][:, src_sl]),
                                reads=[PB[7]], writes=[B_cs])
                prenorm_front(P, S, xk.ap()[0:128, :], 128)
                for t in range(32):
                    k = t % 2
                    prenorm_back(P, S, 128, hT[k], BhT[k], [0, 1, 2, 3])
                    if t + 1 < 32:
                        prenorm_front(P, S, xk.ap()[(t + 1) * 128:(t + 2) * 128, :], 128)
                    for bi, (c0, cn) in enumerate([(0, 512), (512, 512), (1024, 64)]):
                        if bi == 0 and t < 15:
                            continue
                        b = 4 + bi
                        for kc in range(16):
                            P.op("pe", lambda e, b=b, kc=kc, k=k, c0=c0, cn=cn: e.matmul(
                                ps[:, b, 0:cn], lhsT=hT[k][:, kc, :], rhs=w_in[:, kc, c0:c0 + cn],
                                start=(kc == 0), stop=(kc == 15)),
                                reads=[BhT[k][0], BhT[k][1], Bw], writes=[PB[b]])
                    if t > 0:
                        part_b(t - 1)
                    s2, bs2 = st2[k], Bst2[k]
                    doq = t >= 15
                    if doq:
                        P.op("act", lambda e, s2=s2: e.activation(out=jq[:], in_=ps[:, 4, :], func=AF.Square, accum_out=s2[:, 0:1]),
                             reads=[PB[4]], writes=[Bjq, bs2])
                    else:
                        P.op("dve", lambda e, s2=s2: e.memset(s2[:, 0:1], 1.0), writes=[bs2])
                    P.op("act", lambda e, s2=s2: e.activation(out=jq[:], in_=ps[:, 5, :], func=AF.Square, accum_out=s2[:, 1:2]),
                         reads=[PB[5]], writes=[Bjq, bs2])
                    P.op("act", lambda e, s2=s2: e.activation(out=s2[:, 2:4], in_=s2[:, 0:2], func=AF.Sqrt, scale=1.0 / 512,
                                                             bias=S["eps"][:]),
                         reads=[bs2, S["ABb"]], writes=[bs2])
                    P.op("dve", lambda e, s2=s2: e.reciprocal(out=s2[:, 4:6], in_=s2[:, 2:4]), reads=[bs2], writes=[bs2])
                    if doq:
                        P.op("dve", lambda e, s2=s2: e.scalar_tensor_tensor(out=cqf[:], in0=ps[:, 4, :], scalar=s2[:, 4:5], in1=gq[:],
                                                                           op0=ALU.mult, op1=ALU.mult),
                             reads=[PB[4], bs2, Bg], writes=[Bcqf])
                    P.op("dve", lambda e, s2=s2: e.scalar_tensor_tensor(out=ckvf[:], in0=ps[:, 5, :], scalar=s2[:, 5:6], in1=gkv[:],
                                                                       op0=ALU.mult, op1=ALU.mult),
                         reads=[PB[5], bs2, Bg], writes=[Bckvf])
                    P.op("act", lambda e: e.copy(out=krf[:], in_=ps[:, 6, 0:64]), reads=[PB[6]], writes=[Bkrf])
                    P.op("dve", lambda e, t=t: e.tensor_tensor(out=krt[:, 0:32], in0=krf[:, 32:64], in1=SIN2S[:, t, 0:32], op=ALU.mult),
                         reads=[Bkrf, Br], writes=[Bkrf])
                    P.op("dve", lambda e, t=t: e.tensor_tensor(out=krt[:, 32:64], in0=krf[:, 0:32], in1=SIN2S[:, t, 32:64], op=ALU.mult),
                         reads=[Bkrf, Br], writes=[Bkrf])
                    P.op("dve", lambda e, t=t: e.tensor_tensor(out=kro[:], in0=krf[:], in1=COS2[:, t, :], op=ALU.mult),
                         reads=[Bkrf, Br], writes=[Bkrf])
                    P.op("dve", lambda e: e.tensor_tensor(out=kro[:], in0=kro[:], in1=krt[:], op=ALU.add),
                         reads=[Bkrf], writes=[Bkrf])
                part_b(31)
                if debug:
                    P.dma("sp", dbg_lat.ap(), c_kvT[:], reads=[B_ckv])
                    P.dma("sp", dbg_kr.ap(), k_ropeT[0:64, :], reads=[B_kr])
                    P.dma("sp", dbg_cq.ap(), c_qT[:], reads=[B_cq])
                P.emit()
            if stop_after <= 2:
                return nc

            with ExitStack() as ph:
                A = ph.enter_context
                P = Prog(ctx)
                kb = A(sbt("kb", [128, 16], F32))
                cmask = A(sbt("cmask_sb", [128, 4, 512], BF16))
                hmask = A(sbt("hmask_sb", [128, 2], BF16))
                Bm = Buf()
                P.dma("sp", kb[:], kbias_d.ap(), writes=[Bm])
                P.dma("pool", cmask[:], cmask_d.ap(), writes=[Bm])
                P.dma("pool", hmask[:], hmask_d.ap(), writes=[Bm])
                wq = [A(sbt(f"wq{k}", [128, 4, 320], BF16)) for k in range(2)]
                wkv = [A(sbt(f"wkv{k}", [128, 4, 256], BF16)) for k in range(2)]
                KnT = [A(sbt(f"KnT{k}", [128, SEQ], BF16)) for k in range(2)]
                Vaug = [A(sbt(f"Vaug{k}", [128, 32, 129], BF16)) for k in range(2)]
                QnT = [A(sbt(f"QnT{k}", [128, NTOK], BF16)) for k in range(2)]
                QrT = [A(sbt(f"QrT{k}", [128, NTOK], BF16)) for k in range(2)]
                Bwq, Bwkv, BK, BV, BQn, BQr = ([Buf() for _ in range(2)] for _ in range(6))
                qt1 = A(sbt("qt1", [64, 512], F32))
                qt2 = A(sbt("qt2", [64, 512], F32))
                Bqt = Buf()
                NPT = 3
                PT = [A(sbt(f"PT{k}", [128, 512], BF16)) for k in range(NPT)]
                BPT = [Buf() for _ in range(NPT)]
                lst = [A(sbt(f"lst{k}", [128, 2], F32)) for k in range(2)]
                Blst = [Buf() for _ in range(2)]
                of = [A(sbt(f"of{k}", [128, 128], F32)) for k in range(2)]
                Bof = [Buf() for _ in range(2)]
                oTs = [A(sbt(f"oTs{k}", [128, 512], BF16)) for k in range(2)]
                BoTs = [Buf() for _ in range(2)]
                for k in range(2):
                    P.op("dve", lambda e, k=k: e.memset(Vaug[k][:, :, 128:129], 1.0), writes=[BV[k]])
                    P.op("dve", lambda e, k=k: e.memset(QrT[k][64:128, :], 0.0), writes=[BQr[k]])
                wuq = mla_w_uq.ap().rearrange("(kc p) h d -> p kc h d", p=128)
                wukv = mla_w_ukv.ap().rearrange("(kc p) h d -> p kc h d", p=128)
                pcount = [0]
                ocount = [0]
                Mm = mod_setup(ph, 2)
                late = list(mod_jobs_late)
                slot = [0]
                def load_head_weights(h_):
                    k_ = h_ % 2
                    P.dma("pool", wq[k_][:, :, 0:192], wuq[:, :, h_, :], writes=[Bwq[k_]])
                    P.dma("pool", wq[k_][:, :, 192:224], wuq[:, :, h_, 160:192], writes=[Bwq[k_]])
                    P.dma("pool", wq[k_][:, :, 224:256], wuq[:, :, h_, 128:160], writes=[Bwq[k_]])
                    P.dma("pool", wq[k_][:, :, 256:320], wuq[:, :, h_, 128:192], writes=[Bwq[k_]])
                    P.op("dve", lambda e: e.tensor_scalar(out=wq[k_][:, :, 192:224], in0=wq[k_][:, :, 192:224], scalar1=-1.0,
                                                          scalar2=None, op0=ALU.mult), reads=[Bwq[k_]], writes=[Bwq[k_]])
                    P.dma("pool", wkv[k_][:], wukv[:, :, h_, :], writes=[Bwkv[k_]])

                qblocks = [(0, 512), (512, 512), (1024, 512), (1536, 512), (HALF, 2)]

                def proj_K(h):
                    k = h % 2
                    for kbk in range(8):
                        b = 6 + (kbk % 2)
                        for kc in range(4):
                            P.op("pe", lambda e, b=b, kc=kc, kbk=kbk, k=k: e.matmul(
                                ps[:, b, :], lhsT=wkv[k][:, kc, 0:128], rhs=c_kvT[:, kc, kbk * 512:(kbk + 1) * 512],
                                start=(kc == 0), stop=(kc == 3)), reads=[Bwkv[k], B_ckv], writes=[PB[b]])
                        P.op("dve", lambda e, b=b, kbk=kbk, k=k: e.tensor_copy(out=KnT[k][:, kbk * 512:(kbk + 1) * 512], in_=ps[:, b, :]),
                             reads=[PB[b]], writes=[BK[k]])

                def proj_V(h):
                    k = h % 2
                    for g in range(8):
                        b = 6 + (g % 2)
                        for j in range(4):
                            c = 4 * g + j
                            for kc in range(4):
                                P.op("pe", lambda e, b=b, kc=kc, c=c, j=j, k=k: e.matmul(
                                    ps[:, b, j * 128:(j + 1) * 128], lhsT=c_kvT[:, kc, c * 128:(c + 1) * 128],
                                    rhs=wkv[k][:, kc, 128:256], start=(kc == 0), stop=(kc == 3)),
                                    reads=[Bwkv[k], B_ckv], writes=[PB[b]])
                        P.op("dve", lambda e, b=b, g=g, k=k: e.tensor_copy(
                            out=Vaug[k][:, 4 * g:4 * g + 4, 0:128], in_=ps[:, b, :].rearrange("p (j n) -> p j n", j=4)),
                            reads=[PB[b]], writes=[BV[k]])

                def proj_Q(h):
                    k = h % 2
                    for (q0, nq) in qblocks:
                        for kc in range(4):
                            P.op("pe", lambda e, kc=kc, q0=q0, nq=nq, k=k: e.matmul(
                                ps[:, 6, 0:nq], lhsT=wq[k][:, kc, 0:128], rhs=c_qT[:, kc, q0:q0 + nq],
                                start=(kc == 0), stop=(kc == 3)), reads=[Bwq[k], B_cq], writes=[PB[6]])
                        P.op("dve", lambda e, q0=q0, nq=nq, k=k: e.tensor_copy(out=QnT[k][:, q0:q0 + nq], in_=ps[:, 6, 0:nq]),
                             reads=[PB[6]], writes=[BQn[k]])
                        for kc in range(4):
                            P.op("pe", lambda e, kc=kc, q0=q0, nq=nq, k=k: e.matmul(
                                ps[:, 7, 0:nq], lhsT=wq[k][:, kc, 128:256], rhs=c_qT[:, kc, q0:q0 + nq],
                                start=(kc == 0), stop=(kc == 3)), reads=[Bwq[k], B_cq], writes=[PB[7]])
                        P.op("dve", lambda e, q0=q0, nq=nq: e.tensor_tensor(out=qt1[:, 0:nq], in0=ps[0:64, 7, 0:nq],
                                                                            in1=cos2T[:, q0:q0 + nq], op=ALU.mult),
                             reads=[PB[7], B_cs], writes=[Bqt])
                        for kc in range(4):
                            P.op("pe", lambda e, kc=kc, q0=q0, nq=nq, k=k: e.matmul(
                                ps[:, 7, 0:nq], lhsT=wq[k][:, kc, 192:320], rhs=c_qT[:, kc, q0:q0 + nq],
                                start=(kc == 0), stop=(kc == 3)), reads=[Bwq[k], B_cq], writes=[PB[7]])
                        P.op("dve", lambda e, q0=q0, nq=nq: e.tensor_tensor(out=qt2[:, 0:nq], in0=ps[0:64, 7, 0:nq],
                                                                            in1=sin2T[:, q0:q0 + nq], op=ALU.mult),
                             reads=[PB[7], B_cs, Bqt], writes=[Bqt])
                        P.op("dve", lambda e, q0=q0, nq=nq, k=k: e.tensor_tensor(out=QrT[k][0:64, q0:q0 + nq], in0=qt1[:, 0:nq],
                                                                                 in1=qt2[:, 0:nq], op=ALU.add),
                             reads=[Bqt], writes=[BQr[k]])

                load_head_weights(0)
                proj_K(0)
                proj_V(0)
                proj_Q(0)
                for h in range(NH):
                    k = h % 2
                    if h + 1 < NH:
                        load_head_weights(h + 1)
                    for qi, (q0, nq) in enumerate(qblocks):
                        halo = qi == 4
                        chunks = [(c, 0, ("prev", c)) for c in range(16)]
                        if halo:
                            chunks[15] = (15, 0, ("halo", 15))
                        else:
                            for c in range(4 * qi):
                                chunks.append((16 + c, 0, None))
                            for dd in range(4):
                                chunks.append((16 + 4 * qi + dd, 128 * dd, ("diag", dd)))
                        nqt = (nq + 127) // 128
                        started = [False] * 4

                        def emit_S(ci, sb):
                            c, qs, mk = chunks[ci]
                            w = nq - qs
                            diag = mk is not None and mk[0] in ("diag", "halo")
                            P.op("pe", lambda e: e.matmul(ps[:, sb, 0:w], lhsT=KnT[k][:, c * 128:(c + 1) * 128],
                                                          rhs=QnT[k][:, q0 + qs:q0 + nq], start=True, stop=False),
                                 reads=[BK[k], BQn[k]], writes=[PB[sb]])
                            P.op("pe", lambda e: e.matmul(ps[:, sb, 0:w], lhsT=k_ropeT[:, c * 128:(c + 1) * 128],
                                                          rhs=QrT[k][:, q0 + qs:q0 + nq], start=False, stop=not diag),
                                 reads=[B_kr, BQr[k]], writes=[PB[sb]])
                            if diag:
                                mrhs = hmask[:, 0:2] if mk[0] == "halo" else cmask[:, mk[1], qs:512]
                                P.op("pe", lambda e: e.matmul(ps[:, sb, 0:w], lhsT=identb[:], rhs=mrhs, start=False, stop=True),
                                     reads=[Bm, B_const], writes=[PB[sb]])

                        def emit_PV(ci, sb):
                            c, qs, mk = chunks[ci]
                            w = nq - qs
                            pi = pcount[0] % NPT
                            pcount[0] += 1
                            if mk is not None and mk[0] in ("prev", "halo"):
                                bias = kb[:, mk[1]:mk[1] + 1]
                                rd = [PB[sb], Bm]
                            else:
                                bias = 0.0
                                rd = [PB[sb]]
                            P.op("act", lambda e: e.activation(out=PT[pi][:, 0:w], in_=ps[:, sb, 0:w], func=AF.Exp,
                                                               scale=SCALE, bias=bias),
                                 reads=rd, writes=[BPT[pi]])
                            for qt in range(qs // 128, nqt):
                                m = min(128, nq - qt * 128)
                                lo = qt * 128 - qs
                                last = (ci == len(chunks) - 1) or (qt < 3 and (not halo) and ci == len(chunks) - 4 + qt)
                                P.op("pe", lambda e, qt=qt, m=m, lo=lo, last=last: e.matmul(
                                    ps[0:m, qt, 0:129], lhsT=PT[pi][:, lo:lo + m], rhs=Vaug[k][:, c, :],
                                    start=not started[qt], stop=last),
                                    reads=[BPT[pi], BV[k]], writes=[PB[qt]])
                                started[qt] = True

                        n = len(chunks)
                        emit_S(0, 4)
                        for ci in range(n):
                            if ci + 1 < n:
                                emit_S(ci + 1, 4 + ((ci + 1) % 2))
                            emit_PV(ci, 4 + (ci % 2))
                        oi = ocount[0] % 2
                        ocount[0] += 1
                        for qt in range(nqt):
                            m = min(128, nq - qt * 128)
                            li = qt % 2
                            P.op("dve", lambda e, qt=qt, m=m, li=li: e.tensor_scalar(out=lst[li][0:m, 0:1], in0=ps[0:m, qt, 128:129],
                                                                                     scalar1=1e-30, scalar2=None, op0=ALU.max),
                                 reads=[PB[qt]], writes=[Blst[li]])
                            P.op("dve", lambda e, m=m, li=li: e.reciprocal(out=lst[li][0:m, 1:2], in_=lst[li][0:m, 0:1]),
                                 reads=[Blst[li]], writes=[Blst[li]])
                            P.op("dve", lambda e, qt=qt, m=m, li=li: e.tensor_scalar(out=of[li][0:m, :], in0=ps[0:m, qt, 0:128],
                                                                                     scalar1=lst[li][0:m, 1:2], scalar2=None, op0=ALU.mult),
                                 reads=[PB[qt], Blst[li]], writes=[Bof[li]])
                            tb = 6 + (qt % 2)
                            P.op("pe", lambda e, m=m, li=li, tb=tb: e.transpose(out=ps[:, tb, 0:m], in_=of[li][0:m, :],
                                                                                identity=identf[0:m, 0:m]),
                                 reads=[Bof[li], B_const], writes=[PB[tb]])
                            P.op("act", lambda e, qt=qt, m=m, tb=tb, oi=oi: e.copy(out=oTs[oi][:, qt * 128:qt * 128 + m], in_=ps[:, tb, 0:m]),
                                 reads=[PB[tb]], writes=[BoTs[oi]])
                        P.dma("sp", oT_d.ap()[h, :, q0:q0 + nq], oTs[oi][:, 0:nq], reads=[BoTs[oi]])
                        if h + 1 < NH:
                            if qi == 0:
                                proj_K(h + 1)
                            elif qi == 1:
                                proj_V(h + 1)
                            elif qi == 2:
                                proj_Q(h + 1)
                        slot[0] += 1
                        if late and slot[0] % 2 == 0:
                            i_, v_, cb_ = late.pop(0)
                            mod_block(P, Mm, i_, v_, cb_, 6 + (Mm.n % 2))
                assert not late
                P.emit()
        if stop_after <= 3:
            return nc

        with ExitStack() as ph:
            A = ph.enter_context
            P = Prog(ctx)
            S = norm_scratch(ph, P, 0)
            w_o = A(sbt("w_o", [128, 16, D], BF16))
            Bw = Buf()
            load_w_resident(P, w_o, mla_w_o.ap(), D, Bw)
            oTt = [A(sbt(f"oTt{k}", [128, 16, 128], BF16)) for k in range(2)]
            BoTt = [Buf() for _ in range(2)]
            oview = oT_d.ap().rearrange("h p t -> p h t")
            for t in range(17):
                k = t % 2
                nt = 128 if t < 16 else 2
                t0 = t * 128
                P.dma("sp", oTt[k][:, :, 0:nt], oview[:, :, t0:t0 + nt], writes=[BoTt[k]])
                b0 = 4 * (t % 2)
                for cb in range(4):
                    for h in range(NH):
                        P.op("pe", lambda e, cb=cb, h=h, k=k, nt=nt: e.matmul(
                            ps[0:nt, b0 + cb, :], lhsT=oTt[k][:, h, 0:nt], rhs=w_o[:, h, cb * 512:(cb + 1) * 512],
                            start=(h == 0), stop=(h == NH - 1)), reads=[BoTt[k], Bw], writes=[PB[b0 + cb]])
                xsrc = xk.ap()[HALF + t0:HALF + t0 + 128, :] if t < 16 else xk.ap()[HALF - 2:HALF, :]
                postnorm(P, S, ps[0:nt, b0:b0 + 4, :].rearrange("p a b -> p (a b)"), PB[b0:b0 + 4], nt, xsrc,
                         xr.ap()[t0:t0 + nt, :], dst_bufs=[XR[t]])
            if debug:
                P.dma("sp", dbg_x1.ap(), xr.ap(), reads=XR)
            P.emit()
        if stop_after <= 4:
            return nc

        def mlp_phase(layer, groups, dst, dbg=None):
            with ExitStack() as ph:
                A = ph.enter_context
                P = Prog(ctx)
                S = norm_scratch(ph, P, 2 * layer + 1)
                hT = A(sbt("hTm", [128, 16, 512], BF16))
                aT = A(sbt("aTm", [128, 64, 512], BF16))
                ysb = A(sbt("ysb", [128, 4, D], F32))
                NWU, NWD = 3, 3
                wu = [A(sbt(f"wu{k}", [128, 16, 256], BF16)) for k in range(NWU)]
                wd = [A(sbt(f"wd{k}", [128, 8, 512], BF16)) for k in range(NWD)]
                rl = [A(sbt(f"rl{k}", [128, 512], F32)) for k in range(2)]
                BhT, BaT = (Buf(), Buf()), Buf()
                Bys = [Buf() for _ in range(4)]
                Bwu = [Buf() for _ in range(NWU)]
                Bwd = [Buf() for _ in range(NWD)]
                Brl = [Buf() for _ in range(2)]
                wup = mlp_w_up.ap()[layer].rearrange("(kc p) n -> p kc n", p=128)
                wdn = mlp_w_down.ap()[layer].rearrange("(fb p) n -> p fb n", p=128)
                nu = 0
                nd = 0
                nr = 0
                Bsu = [Buf() for _ in range(32)]
                Bsd = [Buf() for _ in range(32)]
                def do_prenorm(tiles_):
                    col_ = 0
                    for (r0_, nt_, xi_, d0_) in tiles_:
                        prenorm(P, S, xr.ap()[r0_:r0_ + nt_, :], nt_, hT[:, :, col_:col_ + nt_], BhT, [0, 1, 2, 3])
                        col_ += nt_

                do_prenorm(groups[0])
                for gi, (tiles) in enumerate(groups):
                    ntg = sum(t[1] for t in tiles)
                    for ub in range(32):
                        k = nu % NWU
                        nu += 1
                        if gi == 0:
                            P.dma("pool", wu[k][:], wup[:, :, ub * 256:(ub + 1) * 256], writes=[Bwu[k]])
                            P.dma("sp", wu_bf.ap()[ub], wu[k][:].rearrange("p a b -> p (a b)"), reads=[Bwu[k]], writes=[Bsu[ub]])
                        else:
                            P.dma("sp", wu[k][:].rearrange("p a b -> p (a b)"), wu_bf.ap()[ub], reads=[Bsu[ub]], writes=[Bwu[k]])
                        for sbk in range(2):
                            fb = ub * 2 + sbk
                            b = 4 + (fb % 4)
                            for kc in range(16):
                                P.op("pe", lambda e, b=b, kc=kc, k=k, sbk=sbk: e.matmul(
                                    ps[:, b, 0:ntg], lhsT=wu[k][:, kc, sbk * 128:(sbk + 1) * 128], rhs=hT[:, kc, 0:ntg],
                                    start=(kc == 0), stop=(kc == 15)), reads=[Bwu[k], BhT[0], BhT[1]], writes=[PB[b]])
                            ri = nr % 2
                            nr += 1
                            P.op("act", lambda e, b=b, ri=ri: e.activation(out=rl[ri][:, 0:ntg], in_=ps[:, b, 0:ntg], func=AF.Relu),
                                 reads=[PB[b]], writes=[Brl[ri]])
                            P.op("dve", lambda e, fb=fb, ri=ri: e.tensor_tensor(out=aT[:, fb, 0:ntg], in0=rl[ri][:, 0:ntg],
                                                                                in1=rl[ri][:, 0:ntg], op=ALU.mult),
                                 reads=[Brl[ri]], writes=[BaT])
                    for cb in range(4):
                        bset = 4 * (cb % 2)
                        for fg in range(8):
                            k = nd % NWD
                            nd += 1
                            si = cb * 8 + fg
                            if gi == 0:
                                P.dma("pool", wd[k][:], wdn[:, fg * 8:(fg + 1) * 8, cb * 512:(cb + 1) * 512], writes=[Bwd[k]])
                                P.dma("sp", wd_bf.ap()[si], wd[k][:].rearrange("p a b -> p (a b)"), reads=[Bwd[k]], writes=[Bsd[si]])
                            else:
                                P.dma("sp", wd[k][:].rearrange("p a b -> p (a b)"), wd_bf.ap()[si], reads=[Bsd[si]], writes=[Bwd[k]])
                            col = 0
                            for ti, (r0, nt, xi, d0) in enumerate(tiles):
                                for j in range(8):
                                    fb = fg * 8 + j
                                    P.op("pe", lambda e, ti=ti, nt=nt, col=col, fb=fb, j=j, k=k, bset=bset: e.matmul(
                                        ps[0:nt, bset + ti, :], lhsT=aT[:, fb, col:col + nt], rhs=wd[k][:, j, :],
                                        start=(fb == 0), stop=(fb == 63)), reads=[BaT, Bwd[k]], writes=[PB[bset + ti]])
                                col += nt
                        for ti, (r0, nt, xi, d0) in enumerate(tiles):
                            P.op("dve", lambda e, ti=ti, nt=nt, cb=cb, bset=bset: e.tensor_copy(
                                out=ysb[0:nt, ti, cb * 512:(cb + 1) * 512], in_=ps[0:nt, bset + ti, :]),
                                reads=[PB[bset + ti]], writes=[Bys[ti]])
                        if cb == 1 and gi + 1 < len(groups):
                            do_prenorm(groups[gi + 1])
                    for ti, (r0, nt, xi, d0) in enumerate(tiles):
                        if dst is None:
                            postnorm(P, S, ysb[0:nt, ti, :], [Bys[ti]], nt, xr.ap()[r0:r0 + nt, :], xr.ap()[r0:r0 + nt, :],
                                     dst_bufs=[XR[xi]])
                        elif d0 is not None:
                            postnorm(P, S, ysb[0:nt, ti, :], [Bys[ti]], nt, xr.ap()[r0:r0 + nt, :], dst.ap()[d0:d0 + nt, :])
                if dbg is not None:
                    P.dma("sp", dbg.ap(), xr.ap()[0:dbg.shape[0], :], reads=XR)
                P.emit()

        own_groups = [[(g * 512 + j * 128, 128, g * 4 + j, g * 512 + j * 128) for j in range(4)] for g in range(4)]
        halo_group = [[(HALF, 2, 16, None)]]
        mlp_phase(0, own_groups + halo_group, None, dbg=dbg_x2 if debug else None)
        if stop_after <= 5:
            return nc

        with ExitStack() as ph:
            A = ph.enter_context
            P = Prog(ctx)
            S = norm_scratch(ph, P, 2)
            w_out = A(sbt("w_out", [128, 16, D], BF16))
            Bw = Buf()
            load_w_resident(P, w_out, conv_w_out.ap(), D, Bw)
            cw = A(sbt("cw", [128, 16, 3], F32))
            hf = A(sbt("hf", [128, 1], F32))
            Bcw = Buf()
            P.dma("sp", cw[:], conv_wT.ap(), writes=[Bcw])
            P.dma("sp", hf[:], hflag_d.ap(), writes=[Bcw])
            hT = A(sbt("hTc", [128, 16, 514], BF16))
            gT = A(sbt("gTc", [128, 16, 512], BF16))
            BhT, BgT = (Buf(), Buf()), Buf()
            NWI = 2
            wi = [A(sbt(f"wi{k}", [128, 16, 3, 256], BF16)) for k in range(NWI)]
            Bwi = [[Buf() for _ in range(3)] for _ in range(NWI)]
            zb2 = [A(sbt(f"zb{m}", [128, 514], F32)) for m in range(2)]
            zb = [zb2[m % 2] for m in range(16)]
            Bzb2 = [Buf() for _ in range(2)]
            Bzb = [Bzb2[m % 2] for m in range(16)]
            carry = A(sbt("carry", [128, 16, 2], F32))
            Bcar = Buf()
            csb = [A(sbt(f"csb{k}", [128, 514], F32)) for k in range(2)]
            acc = [A(sbt(f"acc{k}", [128, 512], F32)) for k in range(2)]
            Bcsb = [Buf() for _ in range(2)]
            Bacc = [Buf() for _ in range(2)]
            wiv = conv_w_in.ap().rearrange("(kc p) (s n) -> p kc s n", p=128, s=3)
            nw = 0
            Bsi = [Buf() for _ in range(8)]
            def conv_prenorm(g_):
                if g_ == 0:
                    prenorm(P, S, xr.ap()[HALF:HALF + 2, :], 2, hT[:, :, 512:514], BhT, [0, 1, 2, 3])
                for j_ in range(4):
                    r0_ = g_ * 512 + j_ * 128
                    prenorm(P, S, xr.ap()[r0_:r0_ + 128, :], 128, hT[:, :, j_ * 128:(j_ + 1) * 128], BhT, [0, 1, 2, 3])

            conv_prenorm(0)
            for g in range(4):
                for m in range(16):
                    mi = m % 2
                    if mi == 0:
                        k = nw % NWI
                        nw += 1
                        if g == 0:
                            for s in range(3):
                                P.dma("pool", wi[k][:, :, s, :], wiv[:, :, s, m * 128:m * 128 + 256], writes=[Bwi[k][s]])
                            P.dma("sp", wi_bf.ap()[m // 2], wi[k][:].rearrange("p a b c -> p (a b c)"), reads=Bwi[k], writes=[Bsi[m // 2]])
                        else:
                            P.dma("sp", wi[k][:].rearrange("p a b c -> p (a b c)"), wi_bf.ap()[m // 2], reads=[Bsi[m // 2]], writes=Bwi[k])
                    pb = [4, 5, 6] if m % 2 == 0 else [7, 0, 1]
                    hb_ = 2 + (m % 2)
                    for s in range(3):
                        for kc in range(16):
                            P.op("pe", lambda e, s=s, kc=kc, k=k, b=pb[s]: e.matmul(
                                ps[:, b, 0:512], lhsT=wi[k][:, kc, s, mi * 128:(mi + 1) * 128], rhs=hT[:, kc, 0:512],
                                start=(kc == 0), stop=(kc == 15)), reads=[Bwi[k][s], BhT[0], BhT[1]], writes=[PB[pb[s]]])
                    if g == 0:
                        for s in (1, 2):
                            for kc in range(16):
                                P.op("pe", lambda e, s=s, kc=kc, k=k: e.matmul(
                                    ps[:, hb_, 2 * (s - 1):2 * s], lhsT=wi[k][:, kc, s, mi * 128:(mi + 1) * 128],
                                    rhs=hT[:, kc, 512:514],
                                    start=(kc == 0), stop=(kc == 15)), reads=[Bwi[k][s], BhT[0], BhT[1]], writes=[PB[hb_]])
                    ci = m % 2
                    if g > 0:
                        P.op("act", lambda e, m=m: e.copy(out=zb[m][:, 0:2], in_=carry[:, m, :]),
                             reads=[Bcar], writes=[Bzb[m]])
                    P.op("act", lambda e, ci=ci, b=pb[1]: e.copy(out=csb[ci][:, 2:514], in_=ps[:, b, 0:512]),
                         reads=[PB[pb[1]]], writes=[Bcsb[ci]])
                    P.op("dve", lambda e, ci=ci, b=pb[2], m=m: e.tensor_tensor(
                        out=zb[m][:, 2:514], in0=ps[:, b, 0:512], in1=csb[ci][:, 2:514], op=ALU.mult),
                        reads=[PB[pb[2]], Bcsb[ci]], writes=[Bzb[m]])
                    if g == 0:
                        P.op("act", lambda e, ci=ci: e.copy(out=csb[ci][:, 0:2], in_=ps[:, hb_, 0:2]),
                             reads=[PB[hb_]], writes=[Bcsb[ci]])
                        P.op("dve", lambda e, ci=ci, m=m: e.tensor_tensor(
                            out=zb[m][:, 0:2], in0=ps[:, hb_, 2:4], in1=csb[ci][:, 0:2], op=ALU.mult),
                            reads=[PB[hb_], Bcsb[ci]], writes=[Bzb[m]])
                        P.op("dve", lambda e, m=m: e.tensor_scalar(out=zb[m][:, 0:2], in0=zb[m][:, 0:2], scalar1=hf[:, 0:1],
                                                                   scalar2=None, op0=ALU.mult),
                             reads=[Bzb[m], Bcw], writes=[Bzb[m]])
                    ai = m % 2
                    P.op("dve", lambda e, m=m, ai=ai: e.tensor_scalar(out=acc[ai][:], in0=zb[m][:, 2:514], scalar1=cw[:, m, 2:3],
                                                                      scalar2=None, op0=ALU.mult),
                         reads=[Bzb[m], Bcw], writes=[Bacc[ai]])
                    P.op("dve", lambda e, m=m, ai=ai: e.scalar_tensor_tensor(out=acc[ai][:], in0=zb[m][:, 1:513], scalar=cw[:, m, 1:2],
                                                                             in1=acc[ai][:], op0=ALU.mult, op1=ALU.add),
                         reads=[Bzb[m], Bcw, Bacc[ai]], writes=[Bacc[ai]])
                    P.op("dve", lambda e, m=m, ai=ai: e.scalar_tensor_tensor(out=acc[ai][:], in0=zb[m][:, 0:512], scalar=cw[:, m, 0:1],
                                                                             in1=acc[ai][:], op0=ALU.mult, op1=ALU.add),
                         reads=[Bzb[m], Bcw, Bacc[ai]], writes=[Bacc[ai]])
                    P.op("dve", lambda e, m=m, ai=ai, b=pb[0]: e.tensor_tensor(out=gT[:, m, :], in0=ps[:, b, 0:512], in1=acc[ai][:], op=ALU.mult),
                         reads=[PB[pb[0]], Bacc[ai]], writes=[BgT])
                    if g < 3:
                        P.op("act", lambda e, m=m: e.copy(out=carry[:, m, :], in_=zb[m][:, 512:514]),
                             reads=[Bzb[m]], writes=[Bcar])
                for j in range(4):
                    if j == 2 and g < 3:
                        conv_prenorm(g + 1)
                    r0 = g * 512 + j * 128
                    yb0 = 4 * (j % 2)
                    for cb in range(4):
                        b = yb0 + cb
                        for m in range(16):
                            P.op("pe", lambda e, b=b, m=m, j=j, cb=cb: e.matmul(
                                ps[:, b, :], lhsT=gT[:, m, j * 128:(j + 1) * 128], rhs=w_out[:, m, cb * 512:(cb + 1) * 512],
                                start=(m == 0), stop=(m == 15)), reads=[BgT, Bw], writes=[PB[b]])
                    postnorm(P, S, ps[:, yb0:yb0 + 4, :].rearrange("p a b -> p (a b)"), PB[yb0:yb0 + 4], 128, xr.ap()[r0:r0 + 128, :],
                             xr.ap()[r0:r0 + 128, :], dst_bufs=[XR[g * 4 + j]])
            if debug:
                P.dma("sp", dbg_x3.ap(), xr.ap()[0:HALF, :], reads=XR)
            P.emit()
        if stop_after <= 6:
            return nc

        mlp_phase(1, own_groups, out)
    return nc


def make_in_maps(x, c, positions, w_mod, b_mod, norm_g, mla_w_in, mla_g_q, mla_g_kv, mla_w_uq, mla_w_ukv,
                 mla_w_o, conv_w_in, conv_w, conv_w_out, mlp_w_up, mlp_w_down):
    f32 = np.float32
    x = np.asarray(x, f32)
    c = np.asarray(c, f32)
    positions = np.asarray(positions, np.int32)
    invf = (10000.0 ** (-np.arange(0, 64, 2, dtype=np.float64) / 64) / (2 * np.pi)).astype(f32)
    invf = np.ascontiguousarray(np.broadcast_to(invf[None, :], (128, 32)))
    kk = np.arange(128)[:, None]
    qq = np.arange(512)[None, :]
    cmask = np.stack([np.where(128 * d + kk <= qq, 0.0, NEG) for d in range(4)], axis=1).astype(f32)
    hmask = np.zeros((128, 2), f32)
    hmask[127, 0] = NEG
    shared = {
        "invf": invf, "cmask": np.ascontiguousarray(cmask), "hmask": hmask,
        "w_mod": np.asarray(w_mod, f32), "b_mod": np.asarray(b_mod, f32),
        "norm_g": np.ascontiguousarray(np.asarray(norm_g, f32).reshape(2, 4 * D)),
        "mla_w_in": np.asarray(mla_w_in, f32)[0], "mla_g_q": np.asarray(mla_g_q, f32)[0],
        "mla_g_kv": np.asarray(mla_g_kv, f32)[0], "mla_w_uq": np.asarray(mla_w_uq, f32)[0],
        "mla_w_ukv": np.asarray(mla_w_ukv, f32)[0], "mla_w_o": np.asarray(mla_w_o, f32)[0],
        "conv_w_in": np.asarray(conv_w_in, f32)[0],
        "conv_wT": np.ascontiguousarray(np.asarray(conv_w, f32)[0].reshape(3, 16, 128).transpose(2, 1, 0)),
        "conv_w_out": np.asarray(conv_w_out, f32)[0],
        "mlp_w_up": np.asarray(mlp_w_up, f32), "mlp_w_down": np.asarray(mlp_w_down, f32),
    }
    maps = []
    for core in range(8):
        b, half = core // 2, core % 2
        if half == 0:
            xk = np.concatenate([np.zeros((HALF, D), f32), x[b, :HALF]], axis=0)
            pos = np.concatenate([np.zeros(HALF, np.int32), positions[b, :HALF]])
            kbias = np.full((128, 16), NEG, f32)
            hflag = np.zeros((128, 1), f32)
        else:
            xk = x[b]
            pos = positions[b]
            kbias = np.zeros((128, 16), f32)
            hflag = np.ones((128, 1), f32)
        m = dict(shared)
        m["xk"] = np.ascontiguousarray(xk)
        m["posT"] = np.ascontiguousarray(pos.reshape(32, 128).T)
        m["kbias"] = kbias
        m["hflag"] = hflag
        m["cvec"] = np.ascontiguousarray(c[b].reshape(16, 128).T)
        maps.append(m)
    return maps


def kernel(**inputs):
    maps = make_in_maps(**inputs)
    nc = build()
    res = run_bass_kernel_spmd(nc, maps, core_ids=list(range(8)))
    out = np.empty((4, SEQ, D), np.float32)
    for core in range(8):
        b, half = core // 2, core % 2
        out[b, half * HALF:(half + 1) * HALF] = np.asarray(res.results[core]["out"], np.float32)
    return out
```
